# Optimizing a Trainium2 kernel written in Bass

```python
import math
import jax, jax.numpy as jnp
from jax import lax
import numpy as np

D_MODEL = 2048
BATCH = 4
SEQ = 8192
DEPTH = 1

N_META = 16
LN_EPS = 1e-5
ALPHA = (2 * DEPTH) ** 0.25
BETA = (8 * DEPTH) ** -0.25
POOL_WINDOWS = (2, 4, 8, 16)
N_POOL_GROUPS = len(POOL_WINDOWS)
POOL_GROUP_DIM = D_MODEL // 8
POOL_WIDTH = N_POOL_GROUPS * POOL_GROUP_DIM
SSD_HEAD_DIM = 64
SSD_WIDTH = 3 * D_MODEL // 2
SSD_HEADS = SSD_WIDTH // SSD_HEAD_DIM
SSD_GROUPS = 8
SSD_STATE = 128
CONV_K = 4
CHUNK = 128
DT_MIN = 0.001
DT_MAX = 0.1
CONV_DIM = SSD_WIDTH + 2 * SSD_GROUPS * SSD_STATE
MIX_WIDTH = POOL_WIDTH + SSD_WIDTH
IN_WIDTH = POOL_WIDTH + SSD_WIDTH + CONV_DIM + SSD_HEADS
PEER_HEADS = 8
PEER_NKEYS = 128
PEER_EXPERTS = PEER_NKEYS * PEER_NKEYS
PEER_TOPK = 16
PEER_QDIM = 256
PEER_BLOCK = 256

kernel_name = "hymba_pool_ssd_peer_deepnorm"


def layer_norm(x, g, b):
    x32 = x.astype(jnp.float32)
    mu = jnp.mean(x32, axis=-1, keepdims=True)
    var = jnp.mean(jnp.square(x32 - mu), axis=-1, keepdims=True)
    y = (x32 - mu) * lax.rsqrt(var + LN_EPS) * g.astype(jnp.float32) + b.astype(jnp.float32)
    return y.astype(x.dtype)


def pool_mix(p, w_pool, scale):
    bsz, L, _ = p.shape
    pg = p.astype(jnp.float32).reshape(bsz, L, N_POOL_GROUPS, POOL_GROUP_DIM)
    cs = jnp.concatenate([jnp.zeros_like(pg[:, :1]), jnp.cumsum(pg, axis=1)], axis=1)
    t = jnp.arange(L)
    outs = []
    for gi, w in enumerate(POOL_WINDOWS):
        start = jnp.maximum(t + 1 - w, 0)
        count = (t + 1 - start).astype(jnp.float32)
        win_sum = cs[:, 1:, gi] - cs[:, start, gi]
        outs.append(win_sum / count[None, :, None] - pg[:, :, gi])
    pooled = jnp.stack(outs, axis=2)
    mixed = jnp.einsum('blgc,gcd->blgd', pooled, w_pool.astype(jnp.float32))
    mixed = mixed.reshape(bsz, L, POOL_WIDTH) * scale.astype(jnp.float32)
    return mixed.astype(p.dtype)


def causal_dwconv(u, w, b):
    out = lax.conv_general_dilated(u, w[:, None, :].astype(u.dtype), window_strides=(1,),
                                   padding=[(CONV_K - 1, 0)],
                                   dimension_numbers=('NWC', 'WIO', 'NWC'),
                                   feature_group_count=u.shape[-1])
    return out + b.astype(u.dtype)


def ssd_chunked(x, dt, A, Bm, Cm):
    bsz, L, H, P = x.shape
    G, N = Bm.shape[2], Bm.shape[3]
    R = H // G
    n_pad = (-L) % CHUNK
    pad4 = ((0, 0), (n_pad, 0), (0, 0), (0, 0))
    x = jnp.pad(x, pad4)
    Bm = jnp.pad(Bm, pad4)
    Cm = jnp.pad(Cm, pad4)
    dt = jnp.pad(dt, ((0, 0), (n_pad, 0), (0, 0)))
    Lp = L + n_pad
    nc = Lp // CHUNK
    X = (x * dt[..., None]).reshape(bsz, nc, CHUNK, G, R, P)
    Adt = (dt * A).reshape(bsz, nc, CHUNK, G, R)
    Bc = Bm.reshape(bsz, nc, CHUNK, G, N)
    Cc = Cm.reshape(bsz, nc, CHUNK, G, N)
    xs = tuple(jnp.moveaxis(a, 1, 0) for a in (X, Adt, Bc, Cc))
    causal = jnp.tril(jnp.ones((CHUNK, CHUNK), dtype=bool))[None, :, :, None, None]

    def step(state, inp):
        Xk, Ak, Bk, Ck = inp
        Acs = jnp.cumsum(Ak, axis=1)
        seg = Acs[:, :, None] - Acs[:, None, :]
        Lmat = jnp.exp(jnp.where(causal, seg, -jnp.inf))
        CB = jnp.einsum('blgn,bsgn->blsg', Ck, Bk)
        y_diag = jnp.einsum('blsg,blsgr,bsgrp->blgrp', CB, Lmat, Xk)
        y_off = jnp.einsum('blgn,bgrpn,blgr->blgrp', Ck, state, jnp.exp(Acs))
        decay_end = jnp.exp(Acs[:, -1:] - Acs)
        new_state = state * jnp.exp(Acs[:, -1])[..., None, None] + \
            jnp.einsum('bsgn,bsgr,bsgrp->bgrpn', Bk, decay_end, Xk)
        return new_state, y_diag + y_off

    state0 = jnp.zeros((bsz, G, R, P, N), jnp.float32)
    _, ys = lax.scan(step, state0, xs)
    y = jnp.moveaxis(ys, 0, 1).reshape(bsz, Lp, H, P)
    return y[:, n_pad:]


def ssd_mix(z, xbc, dt_raw, conv_w, conv_b, dt_bias, A_log, D_skip, norm_g):
    bsz, L, _ = z.shape
    xbc = jax.nn.silu(causal_dwconv(xbc, conv_w, conv_b)).astype(jnp.float32)
    xs = xbc[..., :SSD_WIDTH].reshape(bsz, L, SSD_HEADS, SSD_HEAD_DIM)
    Bm = xbc[..., SSD_WIDTH:SSD_WIDTH + SSD_GROUPS * SSD_STATE].reshape(bsz, L, SSD_GROUPS, SSD_STATE)
    Cm = xbc[..., SSD_WIDTH + SSD_GROUPS * SSD_STATE:].reshape(bsz, L, SSD_GROUPS, SSD_STATE)
    dt = jax.nn.softplus(dt_raw.astype(jnp.float32) + dt_bias.astype(jnp.float32))
    A = -jnp.exp(A_log.astype(jnp.float32))
    y = ssd_chunked(xs, dt, A, Bm, Cm)
    y = y + D_skip.astype(jnp.float32)[:, None] * xs
    y = y.reshape(bsz, L, SSD_WIDTH) * jax.nn.silu(z.astype(jnp.float32))
    yg = y.reshape(bsz, L, SSD_GROUPS, SSD_WIDTH // SSD_GROUPS)
    yg = yg * lax.rsqrt(jnp.mean(jnp.square(yg), axis=-1, keepdims=True) + LN_EPS)
    y = yg.reshape(bsz, L, SSD_WIDTH) * norm_g.astype(jnp.float32)
    return y.astype(z.dtype)


def peer_ffn(h, wq, sub_keys, u_tab, v_tab):
    bsz, L, D = h.shape
    T = bsz * L
    n_blk = -(-T // PEER_BLOCK)
    toks = jnp.pad(h.reshape(T, D), ((0, n_blk * PEER_BLOCK - T), (0, 0)))
    toks = toks.reshape(n_blk, PEER_BLOCK, D)

    def block(xb):
        q = (xb @ wq).reshape(PEER_BLOCK, PEER_HEADS, 2, PEER_QDIM // 2)
        s = jnp.einsum('thic,ikc->thik', q, sub_keys).astype(jnp.float32)
        s_top, i_top = lax.top_k(s, PEER_TOPK)
        cand = s_top[:, :, 0, :, None] + s_top[:, :, 1, None, :]
        c_top, c_idx = lax.top_k(cand.reshape(PEER_BLOCK, PEER_HEADS, PEER_TOPK * PEER_TOPK), PEER_TOPK)
        i1 = jnp.take_along_axis(i_top[:, :, 0], c_idx // PEER_TOPK, axis=-1)
        i2 = jnp.take_along_axis(i_top[:, :, 1], c_idx % PEER_TOPK, axis=-1)
        expert = (i1 * PEER_NKEYS + i2).reshape(PEER_BLOCK, PEER_HEADS * PEER_TOPK)
        gate = jax.nn.softmax(c_top, axis=-1).reshape(PEER_BLOCK, PEER_HEADS * PEER_TOPK)
        u = u_tab[expert]
        act = jax.nn.gelu(jnp.einsum('td,tkd->tk', xb, u).astype(jnp.float32), approximate=False)
        a = (gate * act).astype(xb.dtype)
        return jnp.einsum('tk,tkd->td', a, v_tab[expert])

    out = lax.map(block, toks).reshape(n_blk * PEER_BLOCK, D)[:T]
    return out.reshape(bsz, L, D).astype(h.dtype)


def hybrid_layer(h, w_in, pool_w, pool_scale, conv_w, conv_b, dt_bias, A_log, D_skip,
                 ssd_norm_g, w_out, ln1_g, ln1_b, peer_wq, peer_keys, peer_u, peer_v, ln2_g, ln2_b):
    proj = h @ w_in
    o1 = POOL_WIDTH
    o2 = o1 + SSD_WIDTH
    o3 = o2 + CONV_DIM
    p_in = proj[..., :o1]
    z = proj[..., o1:o2]
    xbc = proj[..., o2:o3]
    dt_raw = proj[..., o3:]
    pool_out = pool_mix(p_in, pool_w, pool_scale)
    ssd_out = ssd_mix(z, xbc, dt_raw, conv_w, conv_b, dt_bias, A_log, D_skip, ssd_norm_g)
    mix = jnp.concatenate([pool_out, ssd_out.astype(pool_out.dtype)], axis=-1) @ w_out
    h = layer_norm(ALPHA * h + mix.astype(h.dtype), ln1_g, ln1_b)
    h = layer_norm(ALPHA * h + peer_ffn(h, peer_wq, peer_keys, peer_u, peer_v), ln2_g, ln2_b)
    return h


def setup_inputs(seed: int = 0) -> dict:
    key = jax.random.key(seed)
    ks = jax.random.split(key, 24)
    f32 = jnp.float32

    def nrm(k, shape, s):
        return jax.random.normal(k, shape, f32) * s

    x = nrm(ks[0], (BATCH, SEQ, D_MODEL), 1.0)
    meta_tokens = nrm(ks[1], (N_META, D_MODEL), 1.0)
    ln_in_g = 1.0 + nrm(ks[2], (D_MODEL,), 0.02)
    ln_in_b = nrm(ks[3], (D_MODEL,), 0.02)
    w_in = nrm(ks[4], (DEPTH, D_MODEL, IN_WIDTH), D_MODEL ** -0.5)
    pool_w = nrm(ks[5], (DEPTH, N_POOL_GROUPS, POOL_GROUP_DIM, POOL_GROUP_DIM), POOL_GROUP_DIM ** -0.5)
    pool_scale = 1.0 + nrm(ks[6], (DEPTH, POOL_WIDTH), 0.02)
    conv_w = nrm(ks[7], (DEPTH, CONV_K, CONV_DIM), CONV_K ** -0.5)
    conv_b = nrm(ks[8], (DEPTH, CONV_DIM), 0.01)
    u = jax.random.uniform(ks[9], (DEPTH, SSD_HEADS), f32)
    dt0 = jnp.exp(u * (math.log(DT_MAX) - math.log(DT_MIN)) + math.log(DT_MIN))
    dt_bias = dt0 + jnp.log(-jnp.expm1(-dt0))
    A_log = jnp.log(jax.random.uniform(ks[10], (DEPTH, SSD_HEADS), f32, minval=1.0, maxval=16.0))
    D_skip = 1.0 + nrm(ks[11], (DEPTH, SSD_HEADS), 0.01)
    ssd_norm_g = 1.0 + nrm(ks[12], (DEPTH, SSD_WIDTH), 0.02)
    w_out = nrm(ks[13], (DEPTH, MIX_WIDTH, D_MODEL), BETA * MIX_WIDTH ** -0.5)
    ln1_g = 1.0 + nrm(ks[14], (DEPTH, D_MODEL), 0.02)
    ln1_b = nrm(ks[15], (DEPTH, D_MODEL), 0.02)
    peer_wq = nrm(ks[16], (DEPTH, D_MODEL, PEER_HEADS * PEER_QDIM), D_MODEL ** -0.5)
    peer_keys = nrm(ks[17], (DEPTH, 2, PEER_NKEYS, PEER_QDIM // 2), (PEER_QDIM // 2) ** -0.5)
    peer_u = nrm(ks[18], (DEPTH, PEER_EXPERTS, D_MODEL), D_MODEL ** -0.5)
    peer_v = nrm(ks[19], (DEPTH, PEER_EXPERTS, D_MODEL), BETA * PEER_HEADS ** -0.5)
    ln2_g = 1.0 + nrm(ks[20], (DEPTH, D_MODEL), 0.02)
    ln2_b = nrm(ks[21], (DEPTH, D_MODEL), 0.02)
    return {"x": x, "meta_tokens": meta_tokens, "ln_in_g": ln_in_g, "ln_in_b": ln_in_b,
            "w_in": w_in, "pool_w": pool_w, "pool_scale": pool_scale, "conv_w": conv_w,
            "conv_b": conv_b, "dt_bias": dt_bias, "A_log": A_log, "D_skip": D_skip,
            "ssd_norm_g": ssd_norm_g, "w_out": w_out, "ln1_g": ln1_g, "ln1_b": ln1_b,
            "peer_wq": peer_wq, "peer_keys": peer_keys, "peer_u": peer_u, "peer_v": peer_v,
            "ln2_g": ln2_g, "ln2_b": ln2_b}


def reference(x, meta_tokens, ln_in_g, ln_in_b, w_in, pool_w, pool_scale, conv_w, conv_b,
              dt_bias, A_log, D_skip, ssd_norm_g, w_out, ln1_g, ln1_b, peer_wq, peer_keys,
              peer_u, peer_v, ln2_g, ln2_b):
    bsz = x.shape[0]
    meta = jnp.broadcast_to(meta_tokens[None].astype(x.dtype), (bsz, N_META, x.shape[-1]))
    h = jnp.concatenate([meta, x], axis=1)
    h = layer_norm(h, ln_in_g, ln_in_b)
    for l in range(DEPTH):
        h = hybrid_layer(h, w_in[l], pool_w[l], pool_scale[l], conv_w[l], conv_b[l], dt_bias[l],
                         A_log[l], D_skip[l], ssd_norm_g[l], w_out[l], ln1_g[l], ln1_b[l],
                         peer_wq[l], peer_keys[l], peer_u[l], peer_v[l], ln2_g[l], ln2_b[l])
    return h[:, N_META:]
```

```python
from contextlib import ExitStack
import numpy as np
import concourse.bass as bass
import concourse.mybir as mybir
from concourse.bass_utils import run_bass_kernel_spmd

F32 = mybir.dt.float32
BF16 = mybir.dt.bfloat16
AF = mybir.ActivationFunctionType
ALU = mybir.AluOpType

EPOCH = 24000
DMA_EPOCH = 1500
ALPHA = 2.0 ** 0.25
LN_EPS = 1e-5
D = 2048
NKC = 16
O_Z, O_X, O_B, O_C, O_DT = 1024, 4096, 7168, 8192, 9216


class Tk:
    __slots__ = ("t", "w", "r", "name")

    def __init__(self, t, name=""):
        self.t = t
        self.w = None
        self.r = []
        self.name = name

    def __getitem__(self, k):
        return self.t[k]


class Prog:
    ENG = ("pe", "act", "dve", "pool", "sp")

    def __init__(self, nc, es):
        self.nc = nc
        self.es = es
        self.ops = {e: [] for e in self.ENG}
        self.cnt = {e: 0 for e in self.ENG}
        self.waited = {e: {} for e in self.ENG}
        self.sems = {}
        self.dma_cnt = {}
        self.nsem = 0
        self.nops = 0

    def sem(self, key):
        if key not in self.sems:
            self.sems[key] = self.es.enter_context(self.nc.semaphore("s%d" % self.nsem))
            self.nsem += 1
        return self.sems[key]

    def _eng_next(self, e):
        c = self.cnt[e]
        return ("E", e, c // EPOCH), (c % EPOCH) + 1

    def _deps(self, e, reads, writes):
        need = {}

        def add(d, is_raw):
            if d is None:
                return
            k, v = d
            if k[0] == "E" and k[1] == e and not is_raw:
                return
            if self.waited[e].get(k, 0) >= v:
                return
            if need.get(k, 0) < v:
                need[k] = v

        for t in reads:
            add(t.w, True)
        for t in writes:
            add(t.w, e != "pe")
            for d in t.r:
                add(d, False)
        for k, v in need.items():
            self.waited[e][k] = v
        return list(need.items())

    @staticmethod
    def _compact(r):
        m = {}
        for k, v in r:
            if m.get(k, 0) < v:
                m[k] = v
        return list(m.items())

    def _record(self, reads, writes, k, v):
        for t in reads:
            t.r.append((k, v))
            if len(t.r) > 16:
                t.r = self._compact(t.r)
        for t in writes:
            t.w = (k, v)
            t.r = []

    def op(self, e, fn, reads=(), writes=(), inc=True):
        waits = self._deps(e, reads, writes)
        k, v = self._eng_next(e)
        self.nops += 1
        if inc:
            self.sem(k)
            self.cnt[e] += 1
            self.ops[e].append((waits, fn, (k, 1)))
        else:
            assert e == "pe"
            self.ops[e].append((waits, fn, None))
        self._record(reads, writes, k, v)

    def dma(self, q, fn, group, reads=(), writes=()):
        waits = self._deps(q, reads, writes)
        c = self.dma_cnt.get(group, 0)
        k = ("D", group, c // DMA_EPOCH)
        v = ((c % DMA_EPOCH) + 1) * 16
        self.dma_cnt[group] = c + 1
        self.sem(k)
        self.nops += 1
        self.ops[q].append((waits, fn, (k, 16)))
        self._record(reads, writes, k, v)

    def I(self, e, name, *args, reads=(), writes=(), inc=True, **kw):
        self.op(e, lambda eng: getattr(eng, name)(*args, **kw), reads=reads, writes=writes, inc=inc)

    def D(self, q, kw, group, reads=(), writes=()):
        self.dma(q, lambda eng: eng.dma_start(**kw), group, reads=reads, writes=writes)

    def barrier(self):
        last = {}
        for e in self.ENG:
            c = self.cnt[e]
            if c > 0:
                last[("E", e, (c - 1) // EPOCH)] = ((c - 1) % EPOCH) + 1
        for g, c in self.dma_cnt.items():
            if c > 0:
                last[("D", g, (c - 1) // DMA_EPOCH)] = (((c - 1) % DMA_EPOCH) + 1) * 16
        for e in self.ENG:
            waits = []
            for k, v in last.items():
                if k[0] == "E" and k[1] == e:
                    continue
                if self.waited[e].get(k, 0) >= v:
                    continue
                self.waited[e][k] = v
                waits.append((k, v))
            self.ops[e].append((waits, None, None))

    def final_wait(self, e, tiles):
        waits = self._deps(e, tiles, tiles)
        self.ops[e].append((waits, None, None))

    def emit(self, block):
        P = self

        def run(e, eng):
            for waits, fn, inc in P.ops[e]:
                for k, v in waits:
                    eng.wait_ge(P.sems[k], v)
                if fn is None:
                    continue
                ins = fn(eng)
                if inc is not None:
                    ins.then_inc(P.sems[inc[0]], inc[1])

        @block.tensor
        def _(eng):
            run("pe", eng)

        @block.scalar
        def _(eng):
            run("act", eng)

        @block.vector
        def _(eng):
            run("dve", eng)

        @block.gpsimd
        def _(eng):
            run("pool", eng)

        @block.sync
        def _(eng):
            run("sp", eng)


class Rot:
    def __init__(self, tiles):
        self.tiles = tiles
        self.i = 0

    def next(self):
        t = self.tiles[self.i % len(self.tiles)]
        self.i += 1
        return t


def bc(ap, axis, n):
    a = ap.unsqueeze(axis)
    shp = list(a.shape)
    shp[axis] = n
    return a.to_broadcast(shp)


def build(NT, NPRE=0, debug=False, do_peer=True):
    nc = bass.Bass("TRN2", target_bir_lowering=False)
    NTOK = NT * 128

    def din(name, shape, dt=F32):
        return nc.dram_tensor(name, list(shape), dt, kind="ExternalInput").ap()

    x_in = din("x", [NTOK, D])
    meta_in = din("meta", [128, D])
    xpre_in = din("xpre", [max(NPRE, 1) * 128, D])
    flag_in = din("flag", [128, 1])
    ln_in_g = din("ln_in_g", [128, NKC]); ln_in_b = din("ln_in_b", [128, NKC])
    w_in = din("w_in", [D, 9264])
    pool_w = din("pool_w", [4, 256, 256]); pool_scale = din("pool_scale", [128, 8])
    conv_w = din("conv_w", [128, 40, 4]); conv_b = din("conv_b", [128, 40])
    dt_bias = din("dt_bias", [48]); A_log = din("A_log", [48]); D_skip = din("D_skip", [48])
    ssd_norm_g = din("ssd_norm_g", [128, 24])
    w_out = din("w_out", [4096, D])
    ln1_g = din("ln1_g", [128, NKC]); ln1_b = din("ln1_b", [128, NKC])
    ident_in = din("ident", [128, 128])
    tril_in = din("tril", [128, 128])
    triu_in = din("triu", [128, 128])
    mmask_in = din("mmask", [128, 1])
    peer_wq = din("peer_wq", [D, D]); keysT_in = din("keysT", [128, 2, 128])
    peer_u = din("peer_u", [16384, D]); peer_v = din("peer_v", [16384, D])
    ln2_g = din("ln2_g", [D]); ln2_b = din("ln2_b", [D])
    iota_in = din("iota", [128, 128])
    out = nc.dram_tensor("out", [NTOK, D], F32, kind="ExternalOutput").ap()
    dbg = {}
    if debug:
        dbg["h1T"] = nc.dram_tensor("dbg_h1T", [NT, 128, NKC, 128], F32, kind="ExternalOutput").ap()
        dbg["mixT"] = nc.dram_tensor("dbg_mixT", [NT, 128, 32, 128], BF16, kind="ExternalOutput").ap()

    PIECES = []
    for i in range(4):
        PIECES.append(("pool%d" % i, i * 256, 256))
    for g in range(8):
        PIECES.append(("x%d" % g, O_X + g * 384, 384))
        PIECES.append(("b%d" % g, O_B + g * 128, 128))
        PIECES.append(("c%d" % g, O_C + g * 128, 128))
        PIECES.append(("z%d" % g, O_Z + g * 384, 384))
    PIECES.append(("dt", O_DT, 48))
    w1s = {n: Tk(nc.dram_tensor("w1s_" + n, [128, NKC, w], BF16).ap(), "w1s_" + n) for n, c0, w in PIECES}
    w2s = [Tk(nc.dram_tensor("w2s_%d" % j, [128, 32, 128], BF16).ap(), "w2s%d" % j) for j in range(16)]
    h1s = Tk(nc.dram_tensor("h1s", [NT, 128, NKC, 128], F32).ap(), "h1s")
    wqs = [Tk(nc.dram_tensor("wqs_%d" % j, [128, NKC, 128], BF16).ap(), "wqs%d" % j) for j in range(16)]
    out_tk = Tk(out, "out")

    with ExitStack() as es:
        P = Prog(nc, es)
        block = es.enter_context(nc.Block())
        esA = ExitStack()
        cur_es = [esA]

        def sb(n, s, d=F32):
            return Tk(cur_es[0].enter_context(nc.sbuf_tensor("s_" + n, list(s), d)), n)

        def ps(n, s, d=F32):
            return Tk(cur_es[0].enter_context(nc.psum_tensor("p_" + n, list(s), d)), n)

        identF = sb("identF", [128, 128]); identB = sb("identB", [128, 128], BF16)
        triL = sb("triL", [128, 128]); triU = sb("triU", [128, 128]); ones = sb("ones", [128, 128])
        mmask = sb("mmask", [128, 1]); flag = sb("flag", [128, 1])
        g_in = sb("g_in", [128, NKC]); b_in = sb("b_in", [128, NKC])
        g1 = sb("g1", [128, NKC]); b1 = sb("b1", [128, NKC])
        convw = sb("convw", [128, 40, 4]); convb = sb("convb", [128, 40])
        pscale = sb("pscale", [128, 8]); normg = sb("normg", [128, 24])
        poolw = sb("poolw", [128, 8, 256], BF16)
        dtb = sb("dtb", [128, 48]); Aneg = sb("Aneg", [128, 48]); Dsk = sb("Dsk", [128, 48])
        wdt = sb("wdt", [128, NKC, 48], BF16)
        epsc = sb("epsc", [128, 1])

        xt = [sb("xt%d" % i, [128, D]) for i in range(2)]
        xn = sb("xn", [128, D])
        xn8 = xn.t[:].rearrange("p (a b) -> p a b", b=256)
        xn16 = xn.t[:].rearrange("p (a b) -> p a b", b=128)
        st6 = sb("st6", [128, 4, 6]); mv = sb("mv", [128, 2]); rstd = sb("rstd", [128, 1])
        hT0 = [sb("hT0_%d" % i, [128, NKC, 128]) for i in range(2)]
        hTb = [sb("hTb_%d" % i, [128, NKC, 128], BF16) for i in range(2)]
        wbuf = [sb("wbuf%d" % i, [128, NKC, 384], BF16) for i in range(3)]
        wobuf = [sb("wobuf%d" % i, [128, 32, 128], BF16) for i in range(3)]
        pbuf = sb("pbuf", [128, 8, 144]); pw = [sb("pw%d" % i, [128, 2, 144]) for i in range(2)]
        pooled = sb("pooled", [128, 8, 128], BF16)
        phalo = sb("phalo", [128, 8, 16])
        chalo = sb("chalo", [128, 40, 3])
        cb = [sb("cb%d" % i, [128, 131]) for i in range(3)]
        cacc = [sb("cacc%d" % i, [128, 128]) for i in range(3)]
        xsT = [sb("xsT%d" % i, [128, 128]) for i in range(3)]
        xs_g = [sb("xs_g%d" % i, [128, 384]) for i in range(2)]
        BT = [sb("BT%d" % i, [128, 128], BF16) for i in range(2)]
        CT = [sb("CT%d" % i, [128, 128], BF16) for i in range(2)]
        Btok = [sb("Btok%d" % i, [128, 128], BF16) for i in range(2)]
        t48 = sb("t48", [128, 48]); dtv = sb("dtv", [128, 48]); Adt = sb("Adt", [128, 48])
        eA = sb("eA", [128, 48]); dtot = sb("dtot", [128, 48]); dend = sb("dend", [128, 48])
        rhsA = [sb("rhsA%d" % i, [128, 6, 128]) for i in range(2)]
        Lt = [sb("Lt%d" % i, [128, 6, 128]) for i in range(2)]
        CBm = [sb("CBm%d" % i, [128, 128]) for i in range(2)]
        MT = [sb("MT%d" % i, [128, 6, 128], BF16) for i in range(2)]
        Xdt = [sb("Xdt%d" % i, [128, 384], BF16) for i in range(2)]
        Xdec = [sb("Xdec%d" % i, [128, 384], BF16) for i in range(2)]
        y1 = [sb("y1_%d" % i, [128, 384]) for i in range(2)]
        ysk = [sb("ysk%d" % i, [128, 384]) for i in range(2)]
        zs = [sb("zs%d" % i, [128, 384]) for i in range(2)]
        ssq = [sb("ssq%d" % i, [128, 1]) for i in range(2)]
        junk = sb("junk", [128, 384])
        yn = [sb("yn%d" % i, [128, 384], BF16) for i in range(2)]
        state = [sb("state%d" % g, [128, 384]) for g in range(8)]
        stateB = [sb("stateB%d" % g, [128, 384], BF16) for g in range(8)]
        mixT = sb("mixT", [128, 32, 128], BF16)
        vT = sb("vT", [128, NKC, 128])
        lmean = sb("lmean", [128, 128]); lex2 = sb("lex2", [128, 128]); lrstd = sb("lrstd", [128, 128])
        h1T = vT
        pro_f = xt
        pro_b = hTb

        psT = Rot([ps("psT%d" % i, [128, 512]) for i in range(2)])
        psF = Rot([ps("psF%d" % i, [128, 512]) for i in range(2)])
        psSeg = ps("psSeg", [128, 1024])
        psY = Rot([ps("psY%d" % i, [128, 512]) for i in range(2)])

        dq = Rot(["sp", "act"])

        def ld(dst, src, group, q="sp", reads=()):
            P.D(q, dict(out=dst_ap(dst), in_=src), group, reads=list(reads), writes=[dst[0] if isinstance(dst, tuple) else dst])

        def dst_ap(dst):
            return dst[1] if isinstance(dst, tuple) else dst[:]

        ld(identF, ident_in, "c"); ld(triL, tril_in, "c"); ld(triU, triu_in, "c"); ld(mmask, mmask_in, "c"); ld(flag, flag_in, "c")
        ld(g_in, ln_in_g, "c"); ld(b_in, ln_in_b, "c")
        ld(g1, ln1_g, "c"); ld(b1, ln1_b, "c")
        ld(convw, conv_w, "c"); ld(convb, conv_b, "c")
        ld(pscale, pool_scale, "c"); ld(normg, ssd_norm_g, "c")
        for g in range(4):
            for cbk in range(2):
                ld((xn, xn8[:, g * 2 + cbk, :]), pool_w[g, cbk * 128:(cbk + 1) * 128, :], "c")
        ld(dtb, dt_bias.partition_broadcast(128), "c")
        ld(Aneg, A_log.partition_broadcast(128), "c")
        ld(Dsk, D_skip.partition_broadcast(128), "c")
        P.I("dve", "memset", ones[:], 1.0, writes=[ones])
        P.I("dve", "memset", epsc[:], LN_EPS, writes=[epsc])
        P.I("dve", "tensor_copy", out=identB[:], in_=identF[:], reads=[identF], writes=[identB])
        P.I("dve", "tensor_copy", out=poolw[:], in_=xn8, reads=[xn], writes=[poolw])
        P.I("act", "activation", out=Aneg[:], in_=Aneg[:], func=AF.Exp, reads=[Aneg], writes=[Aneg])
        P.I("dve", "tensor_scalar_mul", out=Aneg[:], in0=Aneg[:], scalar1=-1.0, reads=[Aneg], writes=[Aneg])
        for g in range(8):
            P.I("pool", "memset", state[g][:], 0.0, writes=[state[g]])
            P.I("pool", "memset", stateB[g][:], 0.0, writes=[stateB[g]])
        P.I("pool", "memset", chalo[:], 0.0, writes=[chalo])
        P.I("pool", "memset", phalo[:], 0.0, writes=[phalo])

        pi = [0]

        def convert(src_ap, width, dsts):
            i = pi[0]; pi[0] += 1
            f = pro_f[i % 2]; btk = pro_b[i % 2]; b = btk.t[:].rearrange("p a b -> p (a b)")
            q = dq.next()
            P.D(q, dict(out=f[:, 0:width], in_=src_ap), "pro_f%d" % (i % 2), writes=[f])
            eng = "act" if i % 2 == 0 else "dve"
            if eng == "act":
                P.I("act", "copy", out=b[:, 0:width], in_=f[:, 0:width], reads=[f], writes=[btk])
            else:
                P.I("dve", "tensor_copy", out=b[:, 0:width], in_=f[:, 0:width], reads=[f], writes=[btk])
            for (dtk, dap, c0, w) in dsts:
                P.D(q, dict(out=dap, in_=b[:, c0:c0 + w]), "pro_b%d" % (i % 2),
                      reads=[btk], writes=[dtk])

        for kc in range(NKC):
            for (c_lo, c_hi) in ((0, 1024), (1024, 2560), (2560, 4096), (4096, 5632), (5632, 7168), (7168, 8192), (8192, 9264)):
                dsts = []
                for n, c0, w in PIECES:
                    if c0 >= c_lo and c0 + w <= c_hi:
                        dsts.append((w1s[n], w1s[n].t[:, kc, :], c0 - c_lo, w))
                    else:
                        assert c0 + w <= c_lo or c0 >= c_hi, (n, c0, w)
                convert(w_in[kc * 128:(kc + 1) * 128, c_lo:c_hi], c_hi - c_lo, dsts)
        for kc in range(32):
            dsts = [(w2s[j], w2s[j].t[:, kc, :], j * 128, 128) for j in range(16)]
            convert(w_out[kc * 128:(kc + 1) * 128, :], 2048, dsts)
        ld(wdt, w1s["dt"].t, "c", reads=[w1s["dt"]])

        wi = [0]; woi = [0]

        def load_piece(name, width):
            b = wbuf[wi[0] % 3]; wi[0] += 1
            q = "sp"
            P.D(q, dict(out=b[:, :, 0:width], in_=w1s[name].t), "wbuf%d" % ((wi[0] - 1) % 3),
                  reads=[w1s[name]], writes=[b])
            return b

        def fm_block(wb, j, hb):
            pt = psF.next()
            for kc in range(NKC):
                P.I("pe", "matmul", pt[:, 0:128], lhsT=wb[:, kc, j * 128:(j + 1) * 128], rhs=hb[:, kc, :],
                                                     start=(kc == 0), stop=(kc == NKC - 1),
                     reads=[wb, hb], writes=[pt], inc=(kc == NKC - 1))
            return pt

        ci = [0]

        def conv_block(pt, blk, is_meta, out_ap, out_tk, skip_conv=False):
            i = ci[0]; ci[0] += 1
            c = cb[i % 3]; a = cacc[i % 3]
            P.I("act", "copy", out=c[:, 3:131], in_=pt[:, 0:128], reads=[pt], writes=[c])
            if is_meta:
                P.I("pool", "memset", c[:, 3:115], 0.0, writes=[c])
            P.I("pool", "tensor_copy", out=c[:, 0:3], in_=chalo[:, blk, :], reads=[chalo], writes=[c])
            P.I("pool", "tensor_copy", out=chalo[:, blk, :], in_=c[:, 128:131], reads=[c], writes=[chalo])
            if skip_conv:
                return
            P.I("act", "activation", out=a[:], in_=c[:, 3:131], func=AF.Identity,
                                               bias=convb[:, blk:blk + 1], scale=convw[:, blk, 3:4],
                 reads=[c, convb, convw], writes=[a])
            for k in (2, 1, 0):
                P.I("dve", "scalar_tensor_tensor", out=a[:], in0=c[:, k:k + 128], scalar=convw[:, blk, k:k + 1],
                                                                  in1=a[:], op0=ALU.mult, op1=ALU.add,
                     reads=[c, convw, a], writes=[a])
            P.I("act", "activation", out=out_ap, in_=a[:], func=AF.Silu, reads=[a], writes=[out_tk])

        def tile(ti, src_ap, is_meta, full, prefix=False, need_halo=True, xi=0):
            par = ti % 2
            x = xt[par]; h0 = hT0[par]; hb = hTb[par]
            P.D("sp", dict(out=x[:], in_=src_ap), "xt%d" % par, writes=[x])
            for c in range(4):
                P.I("dve", "bn_stats", out=st6[:, c, :], in_=x[:, c * 512:(c + 1) * 512], reads=[x], writes=[st6])
            P.I("dve", "bn_aggr", out=mv[:], in_=st6[:].rearrange("p a b -> p (a b)"), reads=[st6], writes=[mv])
            P.I("act", "activation", out=rstd[:], in_=mv[:, 1:2], func=AF.Sqrt, bias=epsc[:, 0:1], scale=1.0,
                 reads=[mv, epsc], writes=[rstd])
            P.I("dve", "reciprocal", out=rstd[:], in_=rstd[:], reads=[rstd], writes=[rstd])
            P.I("dve", "tensor_scalar", out=xn[:], in0=x[:], scalar1=mv[:, 0:1], scalar2=rstd[:, 0:1],
                                                  op0=ALU.subtract, op1=ALU.mult, reads=[x, mv, rstd], writes=[xn])
            for g4 in range(4):
                pt = psT.next()
                for j in range(4):
                    kc = g4 * 4 + j
                    P.I("pe", "transpose", out=pt[:, j * 128:(j + 1) * 128], in_=xn[:, kc * 128:(kc + 1) * 128],
                                                                 identity=identF[:],
                         reads=[xn, identF], writes=[pt], inc=(j == 3))
                for j in range(4):
                    kc = g4 * 4 + j
                    P.I("act", "activation", out=h0[:, kc, :], in_=pt[:, j * 128:(j + 1) * 128], func=AF.Identity,
                                                                   bias=b_in[:, kc:kc + 1], scale=g_in[:, kc:kc + 1],
                         reads=[pt, b_in, g_in], writes=[h0])
            P.I("pool", "tensor_copy", out=hb[:], in_=h0[:], reads=[h0], writes=[hb])

            pt = psT.next()
            for kc in range(NKC):
                P.I("pe", "matmul", pt[:, 0:48], lhsT=hb[:, kc, :], rhs=wdt[:, kc, :], start=(kc == 0), stop=(kc == NKC - 1),
                     reads=[hb, wdt], writes=[pt], inc=(kc == NKC - 1))
            P.I("dve", "tensor_tensor", out=t48[:], in0=pt[:, 0:48], in1=dtb[:], op=ALU.add, reads=[pt, dtb], writes=[t48])
            P.I("act", "activation", out=t48[:], in_=t48[:], func=AF.Exp, reads=[t48], writes=[t48])
            P.I("act", "activation", out=dtv[:], in_=t48[:], func=AF.Ln, bias=ones[:, 0:1], scale=1.0, reads=[t48, ones], writes=[dtv])
            if is_meta:
                P.I("dve", "tensor_scalar_mul", out=dtv[:], in0=dtv[:], scalar1=mmask[:, 0:1], reads=[dtv, mmask], writes=[dtv])
            if prefix:
                P.I("dve", "tensor_scalar_mul", out=dtv[:], in0=dtv[:], scalar1=flag[:, 0:1], reads=[dtv, flag], writes=[dtv])
            P.I("dve", "tensor_tensor", out=Adt[:], in0=dtv[:], in1=Aneg[:], op=ALU.mult, reads=[dtv, Aneg], writes=[Adt])
            pt = psT.next()
            P.I("pe", "matmul", pt[:, 0:48], lhsT=triL[:], rhs=Adt[:], start=True, stop=True, reads=[triL, Adt], writes=[pt])
            P.I("act", "activation", out=eA[:], in_=pt[:, 0:48], func=AF.Exp, reads=[pt], writes=[eA])
            pt = psT.next()
            P.I("pe", "matmul", pt[:, 0:48], lhsT=ones[:], rhs=Adt[:], start=True, stop=True, reads=[ones, Adt], writes=[pt])
            P.I("act", "activation", out=dtot[:], in_=pt[:, 0:48], func=AF.Exp, reads=[pt], writes=[dtot])
            if not full:
                pt = psT.next()
                P.I("pe", "matmul", pt[:, 0:48], lhsT=triU[:], rhs=Adt[:], start=True, stop=True, reads=[triU, Adt], writes=[pt])
                P.I("act", "activation", out=dend[:], in_=pt[:, 0:48], func=AF.Exp, reads=[pt], writes=[dend])

            for pc in (range(4) if (full or need_halo) else ()):
                wb = load_piece("pool%d" % pc, 256)
                for j in range(2):
                    blk = pc * 2 + j
                    pt = fm_block(wb, j, hb)
                    P.I("act", "copy", out=pbuf[:, blk, 16:144], in_=pt[:, 0:128], reads=[pt], writes=[pbuf])
            if full or need_halo:
                P.I("pool", "tensor_copy", out=pbuf[:, :, 0:16], in_=phalo[:], reads=[phalo], writes=[pbuf])
                P.I("pool", "tensor_copy", out=phalo[:], in_=pbuf[:, :, 128:144], reads=[pbuf], writes=[phalo])
            if full:
                for gi, w in enumerate((2, 4, 8, 16)):
                    src = pbuf[:, gi * 2:gi * 2 + 2, :]
                    cur_t = pbuf; cur = src
                    sh = 1
                    k = 0
                    while sh < w:
                        dstt = pw[k % 2]; k += 1
                        a_ap = cur
                        P.I("pool", "tensor_tensor", out=dstt[:, :, sh:144], in0=a_ap[:, :, sh:144],
                                                                                            in1=a_ap[:, :, 0:144 - sh], op=ALU.add,
                             reads=[cur_t], writes=[dstt])
                        cur_t = dstt; cur = dstt[:]
                        sh *= 2
                    P.I("dve", "scalar_tensor_tensor", out=pooled[:, gi * 2:gi * 2 + 2, :], in0=cur[:, :, 16:144],
                                                                                      scalar=1.0 / w, in1=pbuf[:, gi * 2:gi * 2 + 2, 16:144],
                                                                                      op0=ALU.mult, op1=ALU.subtract,
                         reads=[cur_t, pbuf], writes=[pooled])
                    for db in range(2):
                        pt = psF.next()
                        for cbk in range(2):
                            P.I("pe", "matmul", pt[:, 0:128], lhsT=poolw[:, gi * 2 + cbk, db * 128:(db + 1) * 128],
                                                                                        rhs=pooled[:, gi * 2 + cbk, :], start=(cbk == 0), stop=(cbk == 1),
                                 reads=[poolw, pooled], writes=[pt], inc=(cbk == 1))
                        P.I("act", "activation", out=mixT[:, gi * 2 + db, :], in_=pt[:, 0:128], func=AF.Identity,
                                                                                scale=pscale[:, gi * 2 + db:gi * 2 + db + 1],
                             reads=[pt, pscale], writes=[mixT])

            for g in range(8):
                gp = g % 2
                hs = slice(g * 6, g * 6 + 6)
                wb = load_piece("x%d" % g, 384)
                ptx = psT.next()
                for j in range(3):
                    pt = fm_block(wb, j, hb)
                    xo = xsT[(g * 3 + j) % 3]
                    conv_block(pt, g * 3 + j, is_meta, xo[:], xo)
                    P.I("pe", "transpose", out=ptx[:, j * 128:(j + 1) * 128], in_=xo[:], identity=identF[:],
                         reads=[xo, identF], writes=[ptx])
                xg = xs_g[gp]
                P.I("act", "copy", out=xg[:], in_=ptx[:, 0:384], reads=[ptx], writes=[xg])
                wb = load_piece("b%d" % g, 128)
                pt = fm_block(wb, 0, hb)
                conv_block(pt, 24 + g, is_meta, BT[gp][:], BT[gp])
                ptb = psT.next()
                ptb_bf = ptb[:].bitcast(BF16)
                P.I("pe", "transpose", out=ptb_bf[:, 0:128], in_=BT[gp][:], identity=identB[:],
                     reads=[BT[gp], identB], writes=[ptb])
                P.I("act", "copy", out=Btok[gp][:], in_=ptb_bf[:, 0:128], reads=[ptb], writes=[Btok[gp]])
                if full or need_halo:
                    wb = load_piece("c%d" % g, 128)
                    pt = fm_block(wb, 0, hb)
                    conv_block(pt, 32 + g, is_meta, CT[gp][:], CT[gp], skip_conv=not full)

                rA = rhsA[gp]; L = Lt[gp]
                xd = Xdt[gp]; xe = Xdec[gp]
                P.I("pool", "tensor_tensor", out=xd[:].rearrange("p (a b) -> p a b", b=64),
                    in0=xg[:].rearrange("p (a b) -> p a b", b=64),
                    in1=bc(dtv[:, hs], 2, 64), op=ALU.mult,
                    reads=[xg, dtv], writes=[xd])
                if full:
                    P.I("pool", "tensor_tensor", out=rA[:], in0=bc(Adt[:, hs], 2, 128), in1=bc(triL[:], 1, 6), op=ALU.mult,
                        reads=[Adt, triL], writes=[rA])
                    P.I("pe", "matmul", psSeg[:, 0:512], lhsT=triU[:], rhs=rA[:, 0:4, :].rearrange("p a b -> p (a b)"), start=True, stop=True,
                        reads=[triU, rA], writes=[psSeg], inc=False)
                    P.I("pe", "matmul", psSeg[:, 512:768], lhsT=triU[:], rhs=rA[:, 4:6, :].rearrange("p a b -> p (a b)"), start=True, stop=True,
                        reads=[triU, rA], writes=[psSeg])
                    P.I("act", "activation", out=L[:].rearrange("p a b -> p (a b)"), in_=psSeg[:, 0:768], func=AF.Exp,
                        reads=[psSeg], writes=[L])
                    P.I("dve", "tensor_tensor", out=xe[:].rearrange("p (a b) -> p a b", b=64),
                        in0=xd[:].rearrange("p (a b) -> p a b", b=64),
                        in1=L[:, :, 127:128].to_broadcast([128, 6, 64]), op=ALU.mult,
                        reads=[xd, L], writes=[xe])
                else:
                    P.I("dve", "tensor_tensor", out=xe[:].rearrange("p (a b) -> p a b", b=64),
                        in0=xd[:].rearrange("p (a b) -> p a b", b=64),
                        in1=bc(dend[:, hs], 2, 64), op=ALU.mult,
                        reads=[xd, dend], writes=[xe])
                if full:
                    pcb = psT.next()
                    P.I("pe", "matmul", pcb[:, 0:128], lhsT=BT[gp][:], rhs=CT[gp][:], start=True, stop=True,
                         reads=[BT[gp], CT[gp]], writes=[pcb])
                    cm = CBm[gp]
                    P.I("dve", "tensor_tensor", out=cm[:], in0=pcb[:, 0:128], in1=triL[:], op=ALU.mult,
                         reads=[pcb, triL], writes=[cm])
                    mt = MT[gp]
                    P.I("dve", "tensor_tensor", out=mt[:], in0=L[:], in1=bc(cm[:], 1, 6), op=ALU.mult,
                         reads=[L, cm], writes=[mt])
                    pyd = psY.next()
                    for h in range(6):
                        P.I("pe", "matmul", pyd[:, h * 64:(h + 1) * 64], lhsT=mt[:, h, :], rhs=xd[:, h * 64:(h + 1) * 64],
                                                                                 start=True, stop=True,
                             reads=[mt, xd], writes=[pyd], inc=(h == 5))
                    pyo = psY.next()
                    P.I("pe", "matmul", pyo[:, 0:384], lhsT=CT[gp][:], rhs=stateB[g][:], start=True, stop=True,
                         reads=[CT[gp], stateB[g]], writes=[pyo])
                    ya = y1[gp]; yk = ysk[gp]
                    P.I("dve", "tensor_tensor", out=ya[:].rearrange("p (a b) -> p a b", b=64),
                                                                                in0=pyo[:, 0:384].rearrange("p (a b) -> p a b", b=64),
                                                                                in1=bc(eA[:, hs], 2, 64), op=ALU.mult,
                         reads=[pyo, eA], writes=[ya])
                    P.I("dve", "tensor_tensor", out=ya[:], in0=ya[:], in1=pyd[:, 0:384], op=ALU.add,
                         reads=[ya, pyd], writes=[ya])
                    P.I("pool", "tensor_tensor", out=yk[:].rearrange("p (a b) -> p a b", b=64),
                                                                               in0=xg[:].rearrange("p (a b) -> p a b", b=64),
                                                                               in1=bc(Dsk[:, hs], 2, 64), op=ALU.mult,
                         reads=[xg, Dsk], writes=[yk])
                    P.I("pool", "tensor_tensor", out=ya[:], in0=ya[:], in1=yk[:], op=ALU.add, reads=[ya, yk], writes=[ya])
                pns = psY.next()
                P.I("pe", "matmul", pns[:, 0:384], lhsT=Btok[gp][:], rhs=xe[:], start=True, stop=True,
                     reads=[Btok[gp], xe], writes=[pns])
                sg = state[g]
                P.I("pool", "tensor_tensor", out=sg[:].rearrange("p (a b) -> p a b", b=64),
                                                                    in0=sg[:].rearrange("p (a b) -> p a b", b=64),
                                                                    in1=bc(dtot[:, hs], 2, 64), op=ALU.mult,
                     reads=[sg, dtot], writes=[sg])
                P.I("dve", "tensor_tensor", out=sg[:], in0=sg[:], in1=pns[:, 0:384], op=ALU.add, reads=[sg, pns], writes=[sg])
                P.I("act", "copy", out=stateB[g][:], in_=sg[:], reads=[sg], writes=[stateB[g]])
                if not full:
                    continue
                wb = load_piece("z%d" % g, 384)
                pz = psF.next()
                for kc in range(NKC):
                    P.I("pe", "matmul", pz[:, 0:384], lhsT=hb[:, kc, :], rhs=wb[:, kc, 0:384], start=(kc == 0), stop=(kc == NKC - 1),
                         reads=[hb, wb], writes=[pz], inc=(kc == NKC - 1))
                zz = zs[gp]
                P.I("act", "activation", out=zz[:], in_=pz[:, 0:384], func=AF.Silu, reads=[pz], writes=[zz])
                P.I("dve", "tensor_tensor", out=ya[:], in0=ya[:], in1=zz[:], op=ALU.mult, reads=[ya, zz], writes=[ya])
                sq = ssq[gp]
                P.I("pool", "memset", sq[:], 0.0, writes=[sq])
                P.I("act", "activation", out=junk[:], in_=ya[:], func=AF.Square, accum_out=sq[:], reads=[ya], writes=[junk, sq])
                P.I("act", "activation", out=sq[:], in_=sq[:], func=AF.Sqrt, bias=epsc[:, 0:1], scale=1.0 / 384.0,
                     reads=[sq, epsc], writes=[sq])
                P.I("dve", "reciprocal", out=sq[:], in_=sq[:], reads=[sq], writes=[sq])
                yy = yn[gp]
                P.I("dve", "tensor_scalar_mul", out=yy[:], in0=ya[:], scalar1=sq[:, 0:1], reads=[ya, sq], writes=[yy])
                pty = psT.next()
                pty_bf = pty[:].bitcast(BF16)
                for j in range(3):
                    P.I("pe", "transpose", out=pty_bf[:, j * 128:(j + 1) * 128], in_=yy[:, j * 128:(j + 1) * 128],
                                                                                identity=identB[:],
                         reads=[yy, identB], writes=[pty], inc=(j == 2))
                for j in range(3):
                    blk = g * 3 + j
                    P.I("act", "activation", out=mixT[:, 8 + blk, :], in_=pty_bf[:, j * 128:(j + 1) * 128], func=AF.Identity,
                                                                                   scale=normg[:, blk:blk + 1],
                         reads=[pty, normg], writes=[mixT])
            if not full:
                return
            for dblk in range(16):
                wo = wobuf[woi[0] % 3]; woi[0] += 1
                q = "sp"
                P.D(q, dict(out=wo[:], in_=w2s[dblk].t), "wobuf%d" % ((woi[0] - 1) % 3),
                      reads=[w2s[dblk]], writes=[wo])
                po = psF.next()
                for kc in range(32):
                    P.I("pe", "matmul", po[:, 0:128], lhsT=wo[:, kc, :], rhs=mixT[:, kc, :], start=(kc == 0), stop=(kc == 31),
                         reads=[wo, mixT], writes=[po], inc=(kc == 31))
                P.I("dve", "scalar_tensor_tensor", out=vT[:, dblk, :], in0=h0[:, dblk, :], scalar=ALPHA, in1=po[:, 0:128],
                                                                              op0=ALU.mult, op1=ALU.add,
                     reads=[h0, po], writes=[vT])
            P.I("act", "activation", out=xn16, in_=vT[:], func=AF.Square, reads=[vT], writes=[xn])
            pm = psT.next()
            for kc in range(NKC):
                P.I("pe", "matmul", pm[:, 0:128], lhsT=ones[:], rhs=vT[:, kc, :], start=(kc == 0), stop=(kc == NKC - 1),
                     reads=[ones, vT], writes=[pm], inc=(kc == NKC - 1))
            pq = psT.next()
            for kc in range(NKC):
                P.I("pe", "matmul", pq[:, 0:128], lhsT=ones[:], rhs=xn16[:, kc, :], start=(kc == 0), stop=(kc == NKC - 1),
                     reads=[ones, xn], writes=[pq], inc=(kc == NKC - 1))
            P.I("act", "mul", out=lmean[:], in_=pm[:, 0:128], mul=1.0 / D, reads=[pm], writes=[lmean])
            P.I("act", "mul", out=lex2[:], in_=pq[:, 0:128], mul=1.0 / D, reads=[pq], writes=[lex2])
            P.I("dve", "tensor_tensor", out=lrstd[:], in0=lmean[:], in1=lmean[:], op=ALU.mult, reads=[lmean], writes=[lrstd])
            P.I("dve", "tensor_tensor", out=lrstd[:], in0=lex2[:], in1=lrstd[:], op=ALU.subtract, reads=[lex2, lrstd], writes=[lrstd])
            P.I("act", "activation", out=lrstd[:], in_=lrstd[:], func=AF.Sqrt, bias=epsc[:, 0:1], scale=1.0, reads=[lrstd, epsc], writes=[lrstd])
            P.I("dve", "reciprocal", out=lrstd[:], in_=lrstd[:], reads=[lrstd], writes=[lrstd])
            P.I("dve", "tensor_tensor", out=vT[:], in0=vT[:], in1=bc(lmean[:], 1, NKC), op=ALU.subtract, reads=[vT, lmean], writes=[vT])
            P.I("dve", "tensor_tensor", out=vT[:], in0=vT[:], in1=bc(lrstd[:], 1, NKC), op=ALU.mult, reads=[vT, lrstd], writes=[vT])
            for kc in range(NKC):
                P.I("act", "activation", out=h1T[:, kc, :], in_=vT[:, kc, :], func=AF.Identity, bias=b1[:, kc:kc + 1], scale=g1[:, kc:kc + 1],
                     reads=[vT, b1, g1], writes=[h1T])
            P.D("sp", dict(out=h1s.t[xi], in_=h1T[:]), "h1T", reads=[h1T], writes=[h1s])
            if debug:
                P.D("sp", dict(out=dbg["h1T"][xi], in_=h1T[:]), "dbg", reads=[h1T])
                P.D("sp", dict(out=dbg["mixT"][xi], in_=mixT[:]), "dbg", reads=[mixT])

        tile(0, meta_in, True, False)
        for i in range(NPRE):
            tile(1 + i, xpre_in[i * 128:(i + 1) * 128, :], False, False, prefix=True, need_halo=(i == NPRE - 1))
        for i in range(NT):
            tile(1 + NPRE + i, x_in[i * 128:(i + 1) * 128, :], False, True, xi=i)

        P.barrier()
        esA.close()
        if do_peer:
            Lb = dict(locals())
            phase_b(Lb)
            esB = Lb["_esB2"]
            P.final_wait("sp", [out_tk])
        else:
            P.final_wait("sp", [h1s])
        P.emit(block)
        if do_peer:
            esB.close()
        print("ops:", P.nops, "sems:", P.nsem, {e: len(P.ops[e]) for e in P.ENG})
    return nc


U32 = mybir.dt.uint32
NEG = -1.0e30
TBS = 4


def phase_b(L):
    nc, P, sb, ps, NT = L["nc"], L["P"], L["sb"], L["ps"], L["NT"]
    h1s, wqs, out_tk, out = L["h1s"], L["wqs"], L["out_tk"], L["out"]
    dq = L["dq"]; cur_es = L["cur_es"]
    uts2 = [Tk(nc.dram_tensor("uts2_%d" % j, [128, 2, NKC, 128], BF16).ap(), "uts2_%d" % j) for j in range(64)]
    vss2 = [Tk(nc.dram_tensor("vss2_%d" % j, [128, 2, D], BF16).ap(), "vss2_%d" % j) for j in range(64)]
    gts = Tk(nc.dram_tensor("gts", [32, NT, 128, 4, 128], BF16).ap(), "gts")

    es0 = ExitStack(); cur_es[0] = es0
    identF = sb("b0_identF", [128, 128])
    pf = [sb("b0_pf%d" % i, [128, D]) for i in range(3)]
    pbf = [sb("b0_pbf%d" % i, [128, D], BF16) for i in range(3)]
    ubf = [sb("b0_ubf%d" % i, [128, NKC, 128], BF16) for i in range(3)]
    psT0 = Rot([ps("b0_psT%d" % i, [128, 512]) for i in range(4)])
    P.D("sp", dict(out=identF[:], in_=L["ident_in"]), "cB", writes=[identF])
    pi = [0]

    def convert(src_ap, dsts):
        i = pi[0]; pi[0] += 1
        f = pf[i % 3]; btk = pbf[i % 3]
        q = dq.next()
        P.D(q, dict(out=f[:], in_=src_ap), "pB_f%d" % (i % 3), writes=[f])
        if i % 2 == 0:
            P.I("act", "copy", out=btk[:], in_=f[:], reads=[f], writes=[btk])
        else:
            P.I("dve", "tensor_copy", out=btk[:], in_=f[:], reads=[f], writes=[btk])
        for (dtk, dap, c0, w) in dsts:
            P.D(q, dict(out=dap, in_=btk[:, c0:c0 + w]), "pB_b%d" % (i % 3), reads=[btk], writes=[dtk])

    for kc in range(NKC):
        convert(L["peer_wq"][kc * 128:(kc + 1) * 128, :], [(wqs[j], wqs[j].t[:, kc, :], j * 128, 128) for j in range(16)])
    for k1 in range(128):
        convert(L["peer_v"][k1 * 128:(k1 + 1) * 128, :], [(vss2[k1 // 2], vss2[k1 // 2].t[:, k1 % 2, :], 0, D)])
    for k1 in range(128):
        i = pi[0]; pi[0] += 1
        f = pf[i % 3]; ub = ubf[i % 3]
        q = dq.next()
        P.D(q, dict(out=f[:], in_=L["peer_u"][k1 * 128:(k1 + 1) * 128, :]), "pB_f%d" % (i % 3), writes=[f])
        for g4 in range(4):
            pt = psT0.next()
            for j in range(4):
                kc = g4 * 4 + j
                P.I("pe", "transpose", out=pt[:, j * 128:(j + 1) * 128], in_=f[:, kc * 128:(kc + 1) * 128], identity=identF[:],
                    reads=[f, identF], writes=[pt], inc=(j == 3))
            if g4 % 2 == 0:
                P.I("act", "copy", out=ub[:, g4 * 4:(g4 + 1) * 4, :].rearrange("p a b -> p (a b)"), in_=pt[:], reads=[pt], writes=[ub])
            else:
                P.I("dve", "tensor_copy", out=ub[:, g4 * 4:(g4 + 1) * 4, :].rearrange("p a b -> p (a b)"), in_=pt[:], reads=[pt], writes=[ub])
        P.D(q, dict(out=uts2[k1 // 2].t[:, k1 % 2], in_=ub[:]), "pB_u%d" % (i % 3), reads=[ub], writes=[uts2[k1 // 2]])
    P.barrier()
    es0.close()

    es1 = ExitStack(); cur_es[0] = es1
    identF = sb("b1_identF", [128, 128]); identB = sb("b1_identB", [128, 128], BF16)
    iota = sb("b1_iota", [128, 128]); keysT = sb("b1_keysT", [128, 2, 128])
    NB = 2

    def dbl(n, shp, dt=F32):
        return [sb("b1_%s%d" % (n, i), shp, dt) for i in range(NB)]

    h1T_ = dbl("h1T", [128, NKC, 128]); h1Tb_ = dbl("h1Tb", [128, NKC, 128], BF16)
    qTt_ = dbl("qT", [128, NKC, 128])
    s_sb_ = dbl("s_sb", [128, 16, 128]); swork_ = dbl("swork", [128, 128])
    a_all_ = dbl("a_all", [128, 16, 16]); idx_ = dbl("idx", [128, 8, 16], U32); idxf_ = dbl("idxf", [128, 128])
    cand_ = dbl("cand", [128, 256]); cw1_ = dbl("cw1", [128, 256]); cw2_ = dbl("cw2", [128, 256])
    ct_ = dbl("ct", [128, 8, 24]); tau_ = dbl("tau", [128, 8]); zsum_ = dbl("zsum", [128, 8]); cex_ = dbl("cex", [128, 8, 16])
    thr_ = dbl("thr", [128, 8, 16]); Fa_ = dbl("Fa", [128, 8, 16]); E2_ = dbl("E2", [128, 8, 128])
    FaT_ = dbl("FaT", [128, 128]); idxT_ = dbl("idxT", [128, 128])
    Rs_ = dbl("Rs", [128, 8, 16, 16], BF16); Rm_ = [sb("b1_Rm", [128, 8, 16, 16], BF16)] * NB
    RT_ = dbl("RT", [128, 128, 128], BF16)
    P1T_ = [sb("b1_P1T", [128, 16, 128], BF16)] * NB
    GT_ = [sb("b1_GT", [128, 128, 128], BF16)] * NB
    wqb = [sb("b1_wqb%d" % i, [128, NKC, 128], BF16) for i in range(2)]
    psT_ = [Rot([ps("b1_psT%d_%d" % (b, i), [128, 512]) for i in range(4)]) for b in range(NB)]

    c = "cB"
    P.D("sp", dict(out=identF[:], in_=L["ident_in"]), c, writes=[identF])
    P.D("sp", dict(out=iota[:], in_=L["iota_in"]), c, writes=[iota])
    P.D("sp", dict(out=keysT[:], in_=L["keysT_in"]), c, writes=[keysT])
    P.I("dve", "tensor_copy", out=identB[:], in_=identF[:], reads=[identF], writes=[identB])
    wqi = [0]

    def tileB1(ti):
        b = ti % NB
        h1T = h1T_[b]; h1Tb = h1Tb_[b]; qTt = qTt_[b]; qT = qTt.t; s_sb = s_sb_[b]; swork = swork_[b]
        a_all = a_all_[b]; idx = idx_[b]; idxf = idxf_[b]; ct = ct_[b]; tau = tau_[b]; zsum = zsum_[b]; cex = cex_[b]
        thr = thr_[b]; Fa = Fa_[b]; E2 = E2_[b]; FaT = FaT_[b]; idxT = idxT_[b]; RT = RT_[b]; GT = GT_[b]
        psT = psT_[b]
        P.D("sp", dict(out=h1T[:], in_=h1s.t[ti]), "b1_h1T%d" % b, reads=[h1s], writes=[h1T])
        P.I("pool", "tensor_copy", out=h1Tb[:], in_=h1T[:], reads=[h1T], writes=[h1Tb])
        yield
        for g4 in range(4):
            pq = psT.next()
            for j in range(4):
                qb = g4 * 4 + j
                w = wqb[wqi[0] % 2]; slot = wqi[0] % 2; wqi[0] += 1
                P.D("sp", dict(out=w[:], in_=wqs[qb].t), "b1_wq%d" % slot, reads=[wqs[qb]], writes=[w])
                for kc in range(NKC):
                    P.I("pe", "matmul", pq[:, j * 128:(j + 1) * 128], lhsT=w[:, kc, :], rhs=h1Tb[:, kc, :], start=(kc == 0), stop=(kc == NKC - 1),
                        reads=[w, h1Tb], writes=[pq], inc=(kc == NKC - 1))
                yield
            P.I("act", "copy", out=qT[:, g4 * 4:(g4 + 1) * 4, :].rearrange("p a b -> p (a b)"), in_=pq[:], reads=[pq], writes=[qTt])
        for g4 in range(4):
            pss = psT.next()
            for j in range(4):
                seg = g4 * 4 + j
                P.I("pe", "matmul", pss[:, j * 128:(j + 1) * 128], lhsT=qT[:, seg, :], rhs=keysT[:, seg % 2, :], start=True, stop=True,
                    reads=[qTt, keysT], writes=[pss], inc=(j == 3))
            P.I("act", "copy", out=s_sb[:, g4 * 4:(g4 + 1) * 4, :].rearrange("p a b -> p (a b)"), in_=pss[:], reads=[pss], writes=[s_sb])
            yield
        for seg in range(16):
            h = seg // 2
            P.I("dve", "max", out=a_all[:, seg, 0:8], in_=s_sb[:, seg, :], reads=[s_sb], writes=[a_all])
            if seg % 2 == 0:
                P.I("dve", "max_index", out=idx[:, h, 0:8], in_max=a_all[:, seg, 0:8], in_values=s_sb[:, seg, :], reads=[a_all, s_sb], writes=[idx])
            P.I("dve", "match_replace", out=swork[:], in_to_replace=a_all[:, seg, 0:8], in_values=s_sb[:, seg, :], imm_value=NEG,
                reads=[a_all, s_sb], writes=[swork])
            P.I("dve", "max", out=a_all[:, seg, 8:16], in_=swork[:], reads=[swork], writes=[a_all])
            if seg % 2 == 0:
                P.I("dve", "max_index", out=idx[:, h, 8:16], in_max=a_all[:, seg, 8:16], in_values=swork[:], reads=[a_all, swork], writes=[idx])
            yield
        for h in range(8):
            cd = cand_[b]; c1 = cw1_[b]; c2 = cw2_[b]
            P.I("dve", "tensor_tensor", out=cd[:].rearrange("p (a b) -> p a b", b=16), in0=bc(a_all[:, 2 * h, :], 2, 16), in1=bc(a_all[:, 2 * h + 1, :], 1, 16),
                op=ALU.add, reads=[a_all], writes=[cd])
            P.I("dve", "max", out=ct[:, h, 0:8], in_=cd[:], reads=[cd], writes=[ct])
            P.I("dve", "match_replace", out=c1[:], in_to_replace=ct[:, h, 0:8], in_values=cd[:], imm_value=NEG, reads=[ct, cd], writes=[c1])
            P.I("dve", "max", out=ct[:, h, 8:16], in_=c1[:], reads=[c1], writes=[ct])
            P.I("dve", "match_replace", out=c2[:], in_to_replace=ct[:, h, 8:16], in_values=c1[:], imm_value=NEG, reads=[ct, c1], writes=[c2])
            P.I("dve", "max", out=ct[:, h, 16:24], in_=c2[:], reads=[c2], writes=[ct])
            yield
        P.I("dve", "tensor_tensor", out=tau[:], in0=ct[:, :, 15], in1=ct[:, :, 16], op=ALU.add, reads=[ct], writes=[tau])
        P.I("dve", "tensor_scalar_mul", out=tau[:], in0=tau[:], scalar1=0.5, reads=[tau], writes=[tau])
        P.I("dve", "tensor_tensor", out=cex[:], in0=ct[:, :, 0:16], in1=ct[:, :, 0:1].to_broadcast([128, 8, 16]), op=ALU.subtract, reads=[ct], writes=[cex])
        P.I("act", "activation", out=cex[:], in_=cex[:], func=AF.Exp, reads=[cex], writes=[cex])
        P.I("dve", "tensor_reduce", out=zsum[:], in_=cex[:], axis=mybir.AxisListType.X, op=ALU.add, reads=[cex], writes=[zsum])
        P.I("dve", "reciprocal", out=zsum[:], in_=zsum[:], reads=[zsum], writes=[zsum])
        yield
        a1 = a_all[:].rearrange("p (h i) j -> p h i j", i=2)[:, :, 0, :]
        a2 = a_all[:].rearrange("p (h i) j -> p h i j", i=2)[:, :, 1, :]
        P.I("dve", "tensor_tensor", out=thr[:], in0=bc(tau[:], 2, 16), in1=a1, op=ALU.subtract, reads=[tau, a_all], writes=[thr])
        P.I("dve", "tensor_tensor", out=Fa[:], in0=a1, in1=a1[:, :, 0:1].to_broadcast([128, 8, 16]), op=ALU.subtract, reads=[a_all], writes=[Fa])
        P.I("act", "activation", out=Fa[:], in_=Fa[:], func=AF.Exp, reads=[Fa], writes=[Fa])
        P.I("dve", "tensor_tensor", out=Fa[:], in0=Fa[:], in1=bc(zsum[:], 2, 16), op=ALU.mult, reads=[Fa, zsum], writes=[Fa])
        s2 = s_sb[:].rearrange("p (h i) k -> p h i k", i=2)[:, :, 1, :]
        P.I("dve", "tensor_tensor", out=E2[:], in0=s2, in1=a2[:, :, 0:1].to_broadcast([128, 8, 128]), op=ALU.subtract, reads=[s_sb, a_all], writes=[E2])
        P.I("act", "activation", out=E2[:], in_=E2[:], func=AF.Exp, reads=[E2], writes=[E2])
        P.I("dve", "tensor_copy", out=idxf[:].rearrange("p (h j) -> p h j", j=16), in_=idx[:], reads=[idx], writes=[idxf])
        yield
        ptf = psT.next()
        P.I("pe", "transpose", out=ptf[:, 0:128], in_=Fa[:].rearrange("p h j -> p (h j)"), identity=identF[:], reads=[Fa, identF], writes=[ptf], inc=False)
        P.I("pe", "transpose", out=ptf[:, 128:256], in_=idxf[:], identity=identF[:], reads=[idxf, identF], writes=[ptf])
        P.I("act", "copy", out=FaT[:], in_=ptf[:, 0:128], reads=[ptf], writes=[FaT])
        P.I("act", "copy", out=idxT[:], in_=ptf[:, 128:256], reads=[ptf], writes=[idxT])
        yield
        for ks in range(8):
            rm = Rm_[b]; rs = Rs_[b]
            ksl = slice(ks * 16, (ks + 1) * 16)
            P.I("dve", "tensor_tensor", out=rm[:], in0=s2[:, :, ksl].unsqueeze(2).to_broadcast([128, 8, 16, 16]),
                in1=thr[:].unsqueeze(3).to_broadcast([128, 8, 16, 16]), op=ALU.is_ge, reads=[s_sb, thr], writes=[rm])
            P.I("pool", "tensor_tensor", out=rs[:], in0=rm[:], in1=E2[:, :, ksl].unsqueeze(2).to_broadcast([128, 8, 16, 16]), op=ALU.mult,
                reads=[rm, E2], writes=[rs])
            for half in range(2):
                pr = psT.next()
                prb = pr[:].bitcast(BF16)
                for j in range(8):
                    kk = half * 8 + j
                    P.I("pe", "transpose", out=prb[:, j * 128:(j + 1) * 128], in_=rs[:, :, :, kk].rearrange("p h j -> p (h j)"), identity=identB[:],
                        reads=[rs, identB], writes=[pr], inc=(j == 7))
                k0 = ks * 16 + half * 8
                P.I("dve", "tensor_tensor", out=RT[:, :, k0:k0 + 8].rearrange("p t k -> p k t"), in0=prb[:, 0:1024].rearrange("p (k t) -> p k t", t=128),
                    in1=bc(FaT[:], 1, 8), op=ALU.mult, reads=[pr, FaT], writes=[RT])
                yield
        for ts in range(8):
            p1 = P1T_[b]
            P.I("dve", "tensor_tensor", out=p1[:], in0=bc(iota[:], 1, 16), in1=bc(idxT[:, ts * 16:(ts + 1) * 16], 2, 128), op=ALU.is_equal,
                reads=[iota, idxT], writes=[p1])
            for t4 in range(4):
                pg = psT.next()
                for j in range(4):
                    tl = t4 * 4 + j
                    t = ts * 16 + tl
                    P.I("pe", "matmul", pg[:, j * 128:(j + 1) * 128], lhsT=RT[:, t, :], rhs=p1[:, tl, :], start=True, stop=True,
                        reads=[RT, p1], writes=[pg], inc=(j == 3))
                t0 = ts * 16 + t4 * 4
                P.I("act", "copy", out=GT[:, :, t0:t0 + 4].rearrange("p k t -> p t k"), in_=pg[:].rearrange("p (t k) -> p t k", k=128),
                    reads=[pg], writes=[GT])
                yield
        P.D("sp", dict(out=gts.t[:, ti].rearrange("c p j t -> p c (j t)"), in_=GT[:].rearrange("p (c j) t -> p c (j t)", j=4)), "b1_gt",
            reads=[GT], writes=[gts])
        yield

    HALF = 50
    active = []
    nxt = 0
    while nxt < NT or active:
        if nxt < NT and (not active or (len(active) < NB and active[-1][1] >= HALF)):
            active.append([tileB1(nxt), 0]); nxt += 1
        for a in list(active):
            try:
                next(a[0]); a[1] += 1
            except StopIteration:
                active.remove(a)
    P.barrier()
    es1.close()

    es2 = ExitStack(); cur_es[0] = es2
    TB = TBS * 128
    assert NT % TBS == 0 or NT < TBS
    tbs = min(TBS, NT); TBt = tbs * 128
    identF = sb("b2_identF", [128, 128])
    g2bc = sb("b2_g2bc", [128, D]); b2bc = sb("b2_b2bc", [128, D]); epsc = sb("b2_epsc", [128, 1])
    hst = [sb("b2_hst%d" % i, [128, NKC, 128]) for i in range(2)]
    h1Tb = sb("b2_h1Tb", [128, NKC, TBt], BF16)
    NUB = 4
    ub2 = [sb("b2_ub%d" % i, [128, 2, NKC, 128], BF16) for i in range(NUB)]
    vb2 = [sb("b2_vb%d" % i, [128, 2, D], BF16) for i in range(NUB)]
    gtb = [sb("b2_gtb%d" % i, [128, 4, TBt], BF16) for i in range(3)]
    gS = [sb("b2_gS%d" % i, [128, TBt], BF16) for i in range(4)]
    AT = [sb("b2_AT%d" % i, [128, 4, TBt], BF16) for i in range(2)]
    osb = [sb("b2_osb%d" % i, [128, D]) for i in range(tbs)]
    st6 = sb("b2_st6", [128, 4, 6]); mv = sb("b2_mv", [128, 2]); rstd = sb("b2_rstd", [128, 1])
    pS = [ps("b2_pS%d" % i, [128, 512]) for i in range(4)]
    acc = [ps("b2_acc%d" % i, [128, 512]) for i in range(4)]
    P.D("sp", dict(out=identF[:], in_=L["ident_in"]), c, writes=[identF])
    P.D("sp", dict(out=g2bc[:], in_=L["ln2_g"].partition_broadcast(128)), c, writes=[g2bc])
    P.D("sp", dict(out=b2bc[:], in_=L["ln2_b"].partition_broadcast(128)), c, writes=[b2bc])
    P.I("dve", "memset", epsc[:], LN_EPS, writes=[epsc])
    ui = [0]; gi = [0]

    def macro(mi):
        t_lo = mi * tbs
        for st in range(tbs):
            hs_ = hst[st % 2]
            P.D("sp", dict(out=hs_[:], in_=h1s.t[t_lo + st]), "b2_hst%d" % (st % 2), reads=[h1s], writes=[hs_])
            P.I("pool", "tensor_copy", out=h1Tb[:, :, st * 128:(st + 1) * 128], in_=hs_[:], reads=[hs_], writes=[h1Tb])
            P.I("pool", "memset", osb[st][:], 0.0, writes=[osb[st]])
        for cg in range(32):
            gb = gtb[gi[0] % 3]; gslot = gi[0] % 3; gi[0] += 1
            P.D("sp", dict(out=gb[:].rearrange("p j (s t) -> p j s t", t=128),
                           in_=gts.t[cg, t_lo:t_lo + tbs].rearrange("s p j t -> p j s t")), "b2_gt%d" % gslot, reads=[gts], writes=[gb])
            a_ = AT[cg % 2]
            vpair = []
            for pr_ in range(2):
                pair = cg * 2 + pr_
                u = ub2[ui[0] % NUB]; v = vb2[ui[0] % NUB]; slot = ui[0] % NUB; ui[0] += 1
                P.D("sp", dict(out=u[:], in_=uts2[pair].t), "b2_u%d" % slot, reads=[uts2[pair]], writes=[u])
                P.D("sp", dict(out=v[:], in_=vss2[pair].t), "b2_v%d" % slot, reads=[vss2[pair]], writes=[v])
                vpair.append(v)
                for jj in range(2):
                    j = pr_ * 2 + jj
                    for kc in range(NKC):
                        P.I("pe", "matmul", pS[j][:, 0:TBt], lhsT=u[:, jj, kc, :], rhs=h1Tb[:, kc, :], start=(kc == 0), stop=(kc == NKC - 1),
                            reads=[u, h1Tb], writes=[pS[j]], inc=(kc == NKC - 1))
                    g_ = gS[j]
                    P.I("act", "activation", out=g_[:], in_=pS[j][:, 0:TBt], func=AF.Gelu, reads=[pS[j]], writes=[g_])
                    P.I("pool", "tensor_tensor", out=a_[:, j, :], in0=g_[:], in1=gb[:, j, :], op=ALU.mult, reads=[g_, gb], writes=[a_])
            for st in range(tbs):
                for db in range(4):
                    for j in range(4):
                        P.I("pe", "matmul", acc[db][:], lhsT=a_[:, j, st * 128:(st + 1) * 128], rhs=vpair[j // 2][:, j % 2, db * 512:(db + 1) * 512],
                            start=(j == 0), stop=(j == 3), reads=[a_, vpair[j // 2]], writes=[acc[db]], inc=(j == 3))
                    P.I("dve", "tensor_tensor", out=osb[st][:, db * 512:(db + 1) * 512], in0=osb[st][:, db * 512:(db + 1) * 512], in1=acc[db][:], op=ALU.add,
                        reads=[osb[st], acc[db]], writes=[osb[st]])
        for st in range(tbs):
            o = osb[st]; hs_ = hst[st % 2]
            P.D("sp", dict(out=hs_[:], in_=h1s.t[t_lo + st]), "b2_hst%d" % (st % 2), reads=[h1s], writes=[hs_])
            for g4 in range(4):
                pt = pS[g4]
                for j in range(4):
                    kc = g4 * 4 + j
                    P.I("pe", "transpose", out=pt[:, j * 128:(j + 1) * 128], in_=hs_[:, kc, :], identity=identF[:], reads=[hs_, identF], writes=[pt], inc=(j == 3))
                P.I("dve", "scalar_tensor_tensor", out=o[:, g4 * 512:(g4 + 1) * 512], in0=pt[:], scalar=ALPHA, in1=o[:, g4 * 512:(g4 + 1) * 512],
                    op0=ALU.mult, op1=ALU.add, reads=[pt, o], writes=[o])
            for c4 in range(4):
                P.I("dve", "bn_stats", out=st6[:, c4, :], in_=o[:, c4 * 512:(c4 + 1) * 512], reads=[o], writes=[st6])
            P.I("dve", "bn_aggr", out=mv[:], in_=st6[:].rearrange("p a b -> p (a b)"), reads=[st6], writes=[mv])
            P.I("act", "activation", out=rstd[:], in_=mv[:, 1:2], func=AF.Sqrt, bias=epsc[:, 0:1], scale=1.0, reads=[mv, epsc], writes=[rstd])
            P.I("dve", "reciprocal", out=rstd[:], in_=rstd[:], reads=[rstd], writes=[rstd])
            P.I("dve", "tensor_scalar", out=o[:], in0=o[:], scalar1=mv[:, 0:1], scalar2=rstd[:, 0:1], op0=ALU.subtract, op1=ALU.mult,
                reads=[o, mv, rstd], writes=[o])
            P.I("pool", "tensor_tensor", out=o[:], in0=o[:], in1=g2bc[:], op=ALU.mult, reads=[o, g2bc], writes=[o])
            P.I("pool", "tensor_tensor", out=o[:], in0=o[:], in1=b2bc[:], op=ALU.add, reads=[o, b2bc], writes=[o])
            ti = t_lo + st
            P.D("sp", dict(out=out[ti * 128:(ti + 1) * 128, :], in_=o[:]), "b2_out", reads=[o], writes=[out_tk])

    import os as _os
    for mi in range(0 if _os.environ.get("KB_SKIP_B2") else NT // tbs):
        macro(mi)
    L["_esB2"] = es2


def host_consts():
    k = np.arange(128)
    return {
        "ident": np.eye(128, dtype=np.float32),
        "tril": (k[:, None] <= k[None, :]).astype(np.float32),
        "triu": (k[:, None] > k[None, :]).astype(np.float32),
        "mmask": (k >= 112).astype(np.float32).reshape(128, 1),
        "iota": np.tile(k.astype(np.float32)[None, :], (128, 1)),
    }


def core_inputs(inp, b, t0, NT, NPRE=0):
    f = np.float32
    c = host_consts()
    meta = np.zeros((128, D), f)
    meta[112:] = inp["meta_tokens"]
    if NPRE == 0:
        xpre = np.zeros((128, D), f); flag = 0.0
    elif t0 == 0:
        xpre = np.ascontiguousarray(np.tile(meta, (NPRE, 1))); flag = 0.0
    else:
        assert t0 == NPRE
        xpre = np.ascontiguousarray(inp["x"][b, 0:t0 * 128]); flag = 1.0
    m = {
        "xpre": xpre,
        "flag": np.full((128, 1), flag, f),
        "x": np.ascontiguousarray(inp["x"][b, t0 * 128:(t0 + NT) * 128]),
        "meta": meta,
        "ln_in_g": np.ascontiguousarray(inp["ln_in_g"].reshape(NKC, 128).T),
        "ln_in_b": np.ascontiguousarray(inp["ln_in_b"].reshape(NKC, 128).T),
        "w_in": np.ascontiguousarray(inp["w_in"][0]),
        "pool_w": np.ascontiguousarray(inp["pool_w"][0]),
        "pool_scale": np.ascontiguousarray(inp["pool_scale"][0].reshape(8, 128).T),
        "conv_w": np.ascontiguousarray(inp["conv_w"][0].reshape(4, 40, 128).transpose(2, 1, 0)),
        "conv_b": np.ascontiguousarray(inp["conv_b"][0].reshape(40, 128).T),
        "dt_bias": np.ascontiguousarray(inp["dt_bias"][0]),
        "A_log": np.ascontiguousarray(inp["A_log"][0]),
        "D_skip": np.ascontiguousarray(inp["D_skip"][0]),
        "ssd_norm_g": np.ascontiguousarray(inp["ssd_norm_g"][0].reshape(24, 128).T),
        "w_out": np.ascontiguousarray(inp["w_out"][0]),
        "ln1_g": np.ascontiguousarray(inp["ln1_g"][0].reshape(NKC, 128).T),
        "ln1_b": np.ascontiguousarray(inp["ln1_b"][0].reshape(NKC, 128).T),
        "peer_wq": np.ascontiguousarray(inp["peer_wq"][0]),
        "keysT": np.ascontiguousarray(inp["peer_keys"][0].transpose(2, 0, 1)),
        "peer_u": np.ascontiguousarray(inp["peer_u"][0]),
        "peer_v": np.ascontiguousarray(inp["peer_v"][0]),
        "ln2_g": np.ascontiguousarray(inp["ln2_g"][0]),
        "ln2_b": np.ascontiguousarray(inp["ln2_b"][0]),
    }
    m.update(c)
    return m


N_CORES = 8
NT_CORE = 32
_NC_CACHE = {}


def kernel(**inputs):
    inp = {k: np.asarray(v) for k, v in inputs.items()}
    B, S, _ = inp["x"].shape
    halves = N_CORES // B
    assert halves == 2 and S == 2 * NT_CORE * 128
    if "nc" not in _NC_CACHE:
        _NC_CACHE["nc"] = build(NT_CORE, NPRE=NT_CORE)
    nc = _NC_CACHE["nc"]
    in_maps = []
    for core in range(N_CORES):
        b, hf = core // halves, core % halves
        in_maps.append(core_inputs(inp, b, hf * NT_CORE, NT_CORE, NPRE=NT_CORE))
    res = run_bass_kernel_spmd(nc, in_maps, core_ids=list(range(N_CORES)))
    out = np.empty((B, S, D), np.float32)
    for core in range(N_CORES):
        b, hf = core // halves, core % halves
        out[b, hf * NT_CORE * 128:(hf + 1) * NT_CORE * 128] = res.results[core]["out"]
    return out
```

```python
from contextlib import ExitStack
import numpy as np
import concourse.bass as bass
import concourse.mybir as mybir
from concourse.bass_utils import run_bass_kernel_spmd

F32 = mybir.dt.float32
BF16 = mybir.dt.bfloat16
AF = mybir.ActivationFunctionType
ALU = mybir.AluOpType

EPOCH = 24000
DMA_EPOCH = 1500
ALPHA = 2.0 ** 0.25
LN_EPS = 1e-5
D = 2048
NKC = 16
O_Z, O_X, O_B, O_C, O_DT = 1024, 4096, 7168, 8192, 9216


class Tk:
    __slots__ = ("t", "w", "r", "name")

    def __init__(self, t, name=""):
        self.t = t
        self.w = None
        self.r = []
        self.name = name

    def __getitem__(self, k):
        return self.t[k]


class Prog:
    ENG = ("pe", "act", "dve", "pool", "sp")

    def __init__(self, nc, es):
        self.nc = nc
        self.es = es
        self.ops = {e: [] for e in self.ENG}
        self.cnt = {e: 0 for e in self.ENG}
        self.waited = {e: {} for e in self.ENG}
        self.sems = {}
        self.dma_cnt = {}
        self.nsem = 0
        self.nops = 0

    def sem(self, key):
        if key not in self.sems:
            self.sems[key] = self.es.enter_context(self.nc.semaphore("s%d" % self.nsem))
            self.nsem += 1
        return self.sems[key]

    def _eng_next(self, e):
        c = self.cnt[e]
        return ("E", e, c // EPOCH), (c % EPOCH) + 1

    def _deps(self, e, reads, writes):
        need = {}

        def add(d, is_raw):
            if d is None:
                return
            k, v = d
            if k[0] == "E" and k[1] == e and not is_raw:
                return
            if self.waited[e].get(k, 0) >= v:
                return
            if need.get(k, 0) < v:
                need[k] = v

        for t in reads:
            add(t.w, True)
        for t in writes:
            add(t.w, e != "pe")
            for d in t.r:
                add(d, False)
        for k, v in need.items():
            self.waited[e][k] = v
        return list(need.items())

    @staticmethod
    def _compact(r):
        m = {}
        for k, v in r:
            if m.get(k, 0) < v:
                m[k] = v
        return list(m.items())

    def _record(self, reads, writes, k, v):
        for t in reads:
            t.r.append((k, v))
            if len(t.r) > 16:
                t.r = self._compact(t.r)
        for t in writes:
            t.w = (k, v)
            t.r = []

    def op(self, e, fn, reads=(), writes=(), inc=True):
        waits = self._deps(e, reads, writes)
        k, v = self._eng_next(e)
        self.nops += 1
        if inc:
            self.sem(k)
            self.cnt[e] += 1
            self.ops[e].append((waits, fn, (k, 1)))
        else:
            assert e == "pe"
            self.ops[e].append((waits, fn, None))
        self._record(reads, writes, k, v)

    def dma(self, q, fn, group, reads=(), writes=()):
        waits = self._deps(q, reads, writes)
        c = self.dma_cnt.get(group, 0)
        k = ("D", group, c // DMA_EPOCH)
        v = ((c % DMA_EPOCH) + 1) * 16
        self.dma_cnt[group] = c + 1
        self.sem(k)
        self.nops += 1
        self.ops[q].append((waits, fn, (k, 16)))
        self._record(reads, writes, k, v)

    def I(self, e, name, *args, reads=(), writes=(), inc=True, **kw):
        self.op(e, lambda eng: getattr(eng, name)(*args, **kw), reads=reads, writes=writes, inc=inc)

    def D(self, q, kw, group, reads=(), writes=()):
        self.dma(q, lambda eng: eng.dma_start(**kw), group, reads=reads, writes=writes)

    def barrier(self):
        last = {}
        for e in self.ENG:
            c = self.cnt[e]
            if c > 0:
                last[("E", e, (c - 1) // EPOCH)] = ((c - 1) % EPOCH) + 1
        for g, c in self.dma_cnt.items():
            if c > 0:
                last[("D", g, (c - 1) // DMA_EPOCH)] = (((c - 1) % DMA_EPOCH) + 1) * 16
        for e in self.ENG:
            waits = []
            for k, v in last.items():
                if k[0] == "E" and k[1] == e:
                    continue
                if self.waited[e].get(k, 0) >= v:
                    continue
                self.waited[e][k] = v
                waits.append((k, v))
            self.ops[e].append((waits, None, None))

    def final_wait(self, e, tiles):
        waits = self._deps(e, tiles, tiles)
        self.ops[e].append((waits, None, None))

    def emit(self, block):
        P = self

        def run(e, eng):
            for waits, fn, inc in P.ops[e]:
                for k, v in waits:
                    eng.wait_ge(P.sems[k], v)
                if fn is None:
                    continue
                ins = fn(eng)
                if inc is not None:
                    ins.then_inc(P.sems[inc[0]], inc[1])

        @block.tensor
        def _(eng):
            run("pe", eng)

        @block.scalar
        def _(eng):
            run("act", eng)

        @block.vector
        def _(eng):
            run("dve", eng)

        @block.gpsimd
        def _(eng):
            run("pool", eng)

        @block.sync
        def _(eng):
            run("sp", eng)


class Rot:
    def __init__(self, tiles):
        self.tiles = tiles
        self.i = 0

    def next(self):
        t = self.tiles[self.i % len(self.tiles)]
        self.i += 1
        return t


def bc(ap, axis, n):
    a = ap.unsqueeze(axis)
    shp = list(a.shape)
    shp[axis] = n
    return a.to_broadcast(shp)


def build(NT, NPRE=0, debug=False, do_peer=True):
    nc = bass.Bass("TRN2", target_bir_lowering=False)
    NTOK = NT * 128

    def din(name, shape, dt=F32):
        return nc.dram_tensor(name, list(shape), dt, kind="ExternalInput").ap()

    x_in = din("x", [NTOK, D])
    meta_in = din("meta", [128, D])
    xpre_in = din("xpre", [max(NPRE, 1) * 128, D])
    flag_in = din("flag", [128, 1])
    ln_in_g = din("ln_in_g", [128, NKC]); ln_in_b = din("ln_in_b", [128, NKC])
    w_in = din("w_in", [D, 9264])
    pool_w = din("pool_w", [4, 256, 256]); pool_scale = din("pool_scale", [128, 8])
    conv_w = din("conv_w", [128, 40, 4]); conv_b = din("conv_b", [128, 40])
    dt_bias = din("dt_bias", [48]); A_log = din("A_log", [48]); D_skip = din("D_skip", [48])
    ssd_norm_g = din("ssd_norm_g", [128, 24])
    w_out = din("w_out", [4096, D])
    ln1_g = din("ln1_g", [128, NKC]); ln1_b = din("ln1_b", [128, NKC])
    ident_in = din("ident", [128, 128])
    tril_in = din("tril", [128, 128])
    triu_in = din("triu", [128, 128])
    mmask_in = din("mmask", [128, 1])
    peer_wq = din("peer_wq", [D, D]); keysT_in = din("keysT", [128, 2, 128])
    peer_u = din("peer_u", [16384, D]); peer_v = din("peer_v", [16384, D])
    ln2_g = din("ln2_g", [D]); ln2_b = din("ln2_b", [D])
    iota_in = din("iota", [128, 128])
    out = nc.dram_tensor("out", [NTOK, D], F32, kind="ExternalOutput").ap()
    dbg = {}
    if debug:
        dbg["h1T"] = nc.dram_tensor("dbg_h1T", [NT, 128, NKC, 128], F32, kind="ExternalOutput").ap()
        dbg["mixT"] = nc.dram_tensor("dbg_mixT", [NT, 128, 32, 128], BF16, kind="ExternalOutput").ap()

    PIECES = []
    for i in range(4):
        PIECES.append(("pool%d" % i, i * 256, 256))
    for g in range(8):
        PIECES.append(("x%d" % g, O_X + g * 384, 384))
        PIECES.append(("b%d" % g, O_B + g * 128, 128))
        PIECES.append(("c%d" % g, O_C + g * 128, 128))
        PIECES.append(("z%d" % g, O_Z + g * 384, 384))
    PIECES.append(("dt", O_DT, 48))
    w1s = {n: Tk(nc.dram_tensor("w1s_" + n, [128, NKC, w], BF16).ap(), "w1s_" + n) for n, c0, w in PIECES}
    w2s = [Tk(nc.dram_tensor("w2s_%d" % j, [128, 2, D], BF16).ap(), "w2s%d" % j) for j in range(16)]
    h1s = Tk(nc.dram_tensor("h1s", [NT, 128, NKC, 128], F32).ap(), "h1s")
    wqs = [Tk(nc.dram_tensor("wqs_%d" % j, [128, NKC, 128], BF16).ap(), "wqs%d" % j) for j in range(16)]
    out_tk = Tk(out, "out")

    with ExitStack() as es:
        P = Prog(nc, es)
        block = es.enter_context(nc.Block())
        esA = ExitStack()
        cur_es = [esA]

        def sb(n, s, d=F32):
            return Tk(cur_es[0].enter_context(nc.sbuf_tensor("s_" + n, list(s), d)), n)

        def ps(n, s, d=F32):
            return Tk(cur_es[0].enter_context(nc.psum_tensor("p_" + n, list(s), d)), n)

        identF = sb("identF", [128, 128]); identB = sb("identB", [128, 128], BF16)
        triL = sb("triL", [128, 128]); triU = sb("triU", [128, 128]); ones = sb("ones", [128, 128])
        mmask = sb("mmask", [128, 1]); flag = sb("flag", [128, 1])
        g_in = sb("g_in", [128, NKC]); b_in = sb("b_in", [128, NKC])
        g1 = sb("g1", [128, NKC]); b1 = sb("b1", [128, NKC])
        convw = sb("convw", [128, 40, 4]); convb = sb("convb", [128, 40])
        pscale = sb("pscale", [128, 8]); normg = sb("normg", [128, 24])
        poolw = sb("poolw", [128, 8, 256], BF16)
        dtb = sb("dtb", [128, 48]); Aneg = sb("Aneg", [128, 48]); Dsk = sb("Dsk", [128, 48])
        wdt = sb("wdt", [128, NKC, 48], BF16)
        epsc = sb("epsc", [128, 1])

        xt = [sb("xt%d" % i, [128, D]) for i in range(2)]
        xn = sb("xn", [128, D])
        xn8 = xn.t[:].rearrange("p (a b) -> p a b", b=256)
        xn16 = xn.t[:].rearrange("p (a b) -> p a b", b=128)
        st6 = sb("st6", [128, 4, 6]); mv = sb("mv", [128, 2]); rstd = sb("rstd", [128, 1])
        hT0 = [sb("hT0_%d" % i, [128, NKC, 128]) for i in range(2)]
        hTb = [sb("hTb_%d" % i, [128, NKC, 128], BF16) for i in range(2)]
        wbuf = [sb("wbuf%d" % i, [128, NKC, 384], BF16) for i in range(3)]
        wobuf = [sb("wobuf%d" % i, [128, 2, D], BF16) for i in range(3)]
        pbuf = sb("pbuf", [128, 8, 144]); pw = [sb("pw%d" % i, [128, 2, 144]) for i in range(2)]
        pooled = sb("pooled", [128, 8, 128], BF16)
        phalo = sb("phalo", [128, 8, 16])
        chalo = sb("chalo", [128, 40, 3])
        cb = [sb("cb%d" % i, [128, 131]) for i in range(3)]
        cacc = [sb("cacc%d" % i, [128, 128]) for i in range(3)]
        xsT = [sb("xsT%d" % i, [128, 128]) for i in range(3)]
        xs_g = [sb("xs_g%d" % i, [128, 384]) for i in range(2)]
        BT = [sb("BT%d" % i, [128, 128], BF16) for i in range(2)]
        CT = [sb("CT%d" % i, [128, 128], BF16) for i in range(2)]
        Btok = [sb("Btok%d" % i, [128, 128], BF16) for i in range(2)]
        t48 = sb("t48", [128, 48]); dtv = sb("dtv", [128, 48]); Adt = sb("Adt", [128, 48])
        eA = sb("eA", [128, 48]); dtot = sb("dtot", [128, 48]); dend = sb("dend", [128, 48])
        rhsA = [sb("rhsA%d" % i, [128, 6, 128]) for i in range(2)]
        Lt = [sb("Lt%d" % i, [128, 6, 128]) for i in range(2)]
        CBm = [sb("CBm%d" % i, [128, 128]) for i in range(2)]
        MT = [sb("MT%d" % i, [128, 6, 128], BF16) for i in range(2)]
        Xdt = [sb("Xdt%d" % i, [128, 384], BF16) for i in range(2)]
        Xdec = [sb("Xdec%d" % i, [128, 384], BF16) for i in range(2)]
        y1 = [sb("y1_%d" % i, [128, 384]) for i in range(2)]
        ysk = [sb("ysk%d" % i, [128, 384]) for i in range(2)]
        zs = [sb("zs%d" % i, [128, 384]) for i in range(2)]
        ssq = [sb("ssq%d" % i, [128, 1]) for i in range(2)]
        junk = sb("junk", [128, 384])
        yn = [sb("yn%d" % i, [128, 384], BF16) for i in range(2)]
        state = [sb("state%d" % g, [128, 384]) for g in range(8)]
        stateB = [sb("stateB%d" % g, [128, 384], BF16) for g in range(8)]
        mixT = sb("mixT", [128, 32, 128], BF16)
        vT = sb("vT", [128, NKC, 128])
        lmean = sb("lmean", [128, 128]); lex2 = sb("lex2", [128, 128]); lrstd = sb("lrstd", [128, 128])
        h1T = vT
        pro_f = xt
        pro_b = hTb

        psT = Rot([ps("psT%d" % i, [128, 512]) for i in range(2)])
        psF = Rot([ps("psF%d" % i, [128, 512]) for i in range(2)])
        psSeg = ps("psSeg", [128, 1024])
        psY = Rot([ps("psY%d" % i, [128, 512]) for i in range(2)])

        dq = Rot(["sp", "act"])

        def ld(dst, src, group, q="sp", reads=()):
            P.D(q, dict(out=dst_ap(dst), in_=src), group, reads=list(reads), writes=[dst[0] if isinstance(dst, tuple) else dst])

        def dst_ap(dst):
            return dst[1] if isinstance(dst, tuple) else dst[:]

        ld(identF, ident_in, "c"); ld(triL, tril_in, "c"); ld(triU, triu_in, "c"); ld(mmask, mmask_in, "c"); ld(flag, flag_in, "c")
        ld(g_in, ln_in_g, "c"); ld(b_in, ln_in_b, "c")
        ld(g1, ln1_g, "c"); ld(b1, ln1_b, "c")
        ld(convw, conv_w, "c"); ld(convb, conv_b, "c")
        ld(pscale, pool_scale, "c"); ld(normg, ssd_norm_g, "c")
        for g in range(4):
            for cbk in range(2):
                ld((xn, xn8[:, g * 2 + cbk, :]), pool_w[g, cbk * 128:(cbk + 1) * 128, :], "c")
        ld(dtb, dt_bias.partition_broadcast(128), "c")
        ld(Aneg, A_log.partition_broadcast(128), "c")
        ld(Dsk, D_skip.partition_broadcast(128), "c")
        P.I("dve", "memset", ones[:], 1.0, writes=[ones])
        P.I("dve", "memset", epsc[:], LN_EPS, writes=[epsc])
        P.I("dve", "tensor_copy", out=identB[:], in_=identF[:], reads=[identF], writes=[identB])
        P.I("dve", "tensor_copy", out=poolw[:], in_=xn8, reads=[xn], writes=[poolw])
        P.I("act", "activation", out=Aneg[:], in_=Aneg[:], func=AF.Exp, reads=[Aneg], writes=[Aneg])
        P.I("dve", "tensor_scalar_mul", out=Aneg[:], in0=Aneg[:], scalar1=-1.0, reads=[Aneg], writes=[Aneg])
        for g in range(8):
            P.I("pool", "memset", state[g][:], 0.0, writes=[state[g]])
            P.I("pool", "memset", stateB[g][:], 0.0, writes=[stateB[g]])
        P.I("pool", "memset", chalo[:], 0.0, writes=[chalo])
        P.I("pool", "memset", phalo[:], 0.0, writes=[phalo])

        pi = [0]

        def convert(src_ap, width, dsts):
            i = pi[0]; pi[0] += 1
            f = pro_f[i % 2]; btk = pro_b[i % 2]; b = btk.t[:].rearrange("p a b -> p (a b)")
            q = dq.next()
            P.D(q, dict(out=f[:, 0:width], in_=src_ap), "pro_f%d" % (i % 2), writes=[f])
            eng = "act" if i % 2 == 0 else "dve"
            if eng == "act":
                P.I("act", "copy", out=b[:, 0:width], in_=f[:, 0:width], reads=[f], writes=[btk])
            else:
                P.I("dve", "tensor_copy", out=b[:, 0:width], in_=f[:, 0:width], reads=[f], writes=[btk])
            for (dtk, dap, c0, w) in dsts:
                P.D(q, dict(out=dap, in_=b[:, c0:c0 + w]), "pro_b%d" % (i % 2),
                      reads=[btk], writes=[dtk])

        for kc in range(NKC):
            for (c_lo, c_hi) in ((0, 1024), (1024, 2560), (2560, 4096), (4096, 5632), (5632, 7168), (7168, 8192), (8192, 9264)):
                dsts = []
                for n, c0, w in PIECES:
                    if c0 >= c_lo and c0 + w <= c_hi:
                        dsts.append((w1s[n], w1s[n].t[:, kc, :], c0 - c_lo, w))
                    else:
                        assert c0 + w <= c_lo or c0 >= c_hi, (n, c0, w)
                convert(w_in[kc * 128:(kc + 1) * 128, c_lo:c_hi], c_hi - c_lo, dsts)
        for kc in range(32):
            dsts = [(w2s[kc // 2], w2s[kc // 2].t[:, kc % 2, :], 0, D)]
            convert(w_out[kc * 128:(kc + 1) * 128, :], 2048, dsts)
        ld(wdt, w1s["dt"].t, "c", reads=[w1s["dt"]])

        wi = [0]; woi = [0]

        def load_piece(name, width):
            b = wbuf[wi[0] % 3]; wi[0] += 1
            q = "sp"
            P.D(q, dict(out=b[:, :, 0:width], in_=w1s[name].t), "wbuf%d" % ((wi[0] - 1) % 3),
                  reads=[w1s[name]], writes=[b])
            return b

        def fm_block(wb, j, hb, pt=None, col=0):
            if pt is None:
                pt = psF.next()
            for kc in range(NKC):
                P.I("pe", "matmul", pt[:, col:col + 128], lhsT=wb[:, kc, j * 128:(j + 1) * 128], rhs=hb[:, kc, :],
                                                     start=(kc == 0), stop=(kc == NKC - 1),
                     reads=[wb, hb], writes=[pt], inc=(kc == NKC - 1))
            return pt

        ci = [0]

        def conv_block(pt, blk, is_meta, out_ap, out_tk, skip_conv=False, col=0):
            i = ci[0]; ci[0] += 1
            c = cb[i % 3]; a = cacc[i % 3]
            P.I("act", "copy", out=c[:, 3:131], in_=pt[:, col:col + 128], reads=[pt], writes=[c])
            if is_meta:
                P.I("pool", "memset", c[:, 3:115], 0.0, writes=[c])
            P.I("pool", "tensor_copy", out=c[:, 0:3], in_=chalo[:, blk, :], reads=[chalo], writes=[c])
            P.I("pool", "tensor_copy", out=chalo[:, blk, :], in_=c[:, 128:131], reads=[c], writes=[chalo])
            if skip_conv:
                return
            P.I("act", "activation", out=a[:], in_=c[:, 3:131], func=AF.Identity,
                                               bias=convb[:, blk:blk + 1], scale=convw[:, blk, 3:4],
                 reads=[c, convb, convw], writes=[a])
            for k in (2, 1, 0):
                P.I("dve", "scalar_tensor_tensor", out=a[:], in0=c[:, k:k + 128], scalar=convw[:, blk, k:k + 1],
                                                                  in1=a[:], op0=ALU.mult, op1=ALU.add,
                     reads=[c, convw, a], writes=[a])
            P.I("act", "activation", out=out_ap, in_=a[:], func=AF.Silu, reads=[a], writes=[out_tk])

        def tile(ti, src_ap, is_meta, full, prefix=False, need_halo=True, xi=0):
            par = ti % 2
            x = xt[par]; h0 = hT0[par]; hb = hTb[par]
            P.D("sp", dict(out=x[:], in_=src_ap), "xt%d" % par, writes=[x])
            for c in range(4):
                P.I("dve", "bn_stats", out=st6[:, c, :], in_=x[:, c * 512:(c + 1) * 512], reads=[x], writes=[st6])
            P.I("dve", "bn_aggr", out=mv[:], in_=st6[:].rearrange("p a b -> p (a b)"), reads=[st6], writes=[mv])
            P.I("act", "activation", out=rstd[:], in_=mv[:, 1:2], func=AF.Sqrt, bias=epsc[:, 0:1], scale=1.0,
                 reads=[mv, epsc], writes=[rstd])
            P.I("dve", "reciprocal", out=rstd[:], in_=rstd[:], reads=[rstd], writes=[rstd])
            P.I("dve", "tensor_scalar", out=xn[:], in0=x[:], scalar1=mv[:, 0:1], scalar2=rstd[:, 0:1],
                                                  op0=ALU.subtract, op1=ALU.mult, reads=[x, mv, rstd], writes=[xn])
            for g4 in range(4):
                pt = psT.next()
                for j in range(4):
                    kc = g4 * 4 + j
                    P.I("pe", "transpose", out=pt[:, j * 128:(j + 1) * 128], in_=xn[:, kc * 128:(kc + 1) * 128],
                                                                 identity=identF[:],
                         reads=[xn, identF], writes=[pt], inc=(j == 3))
                for j in range(4):
                    kc = g4 * 4 + j
                    P.I("act", "activation", out=h0[:, kc, :], in_=pt[:, j * 128:(j + 1) * 128], func=AF.Identity,
                                                                   bias=b_in[:, kc:kc + 1], scale=g_in[:, kc:kc + 1],
                         reads=[pt, b_in, g_in], writes=[h0])
            P.I("pool", "tensor_copy", out=hb[:], in_=h0[:], reads=[h0], writes=[hb])

            pt = psT.next()
            for kc in range(NKC):
                P.I("pe", "matmul", pt[:, 0:48], lhsT=hb[:, kc, :], rhs=wdt[:, kc, :], start=(kc == 0), stop=(kc == NKC - 1),
                     reads=[hb, wdt], writes=[pt], inc=(kc == NKC - 1))
            P.I("dve", "tensor_tensor", out=t48[:], in0=pt[:, 0:48], in1=dtb[:], op=ALU.add, reads=[pt, dtb], writes=[t48])
            P.I("act", "activation", out=t48[:], in_=t48[:], func=AF.Exp, reads=[t48], writes=[t48])
            P.I("act", "activation", out=dtv[:], in_=t48[:], func=AF.Ln, bias=ones[:, 0:1], scale=1.0, reads=[t48, ones], writes=[dtv])
            if is_meta:
                P.I("dve", "tensor_scalar_mul", out=dtv[:], in0=dtv[:], scalar1=mmask[:, 0:1], reads=[dtv, mmask], writes=[dtv])
            if prefix:
                P.I("dve", "tensor_scalar_mul", out=dtv[:], in0=dtv[:], scalar1=flag[:, 0:1], reads=[dtv, flag], writes=[dtv])
            P.I("dve", "tensor_tensor", out=Adt[:], in0=dtv[:], in1=Aneg[:], op=ALU.mult, reads=[dtv, Aneg], writes=[Adt])
            pt = psT.next()
            P.I("pe", "matmul", pt[:, 0:48], lhsT=triL[:], rhs=Adt[:], start=True, stop=True, reads=[triL, Adt], writes=[pt])
            P.I("act", "activation", out=eA[:], in_=pt[:, 0:48], func=AF.Exp, reads=[pt], writes=[eA])
            pt = psT.next()
            P.I("pe", "matmul", pt[:, 0:48], lhsT=ones[:], rhs=Adt[:], start=True, stop=True, reads=[ones, Adt], writes=[pt])
            P.I("act", "activation", out=dtot[:], in_=pt[:, 0:48], func=AF.Exp, reads=[pt], writes=[dtot])
            if not full:
                pt = psT.next()
                P.I("pe", "matmul", pt[:, 0:48], lhsT=triU[:], rhs=Adt[:], start=True, stop=True, reads=[triU, Adt], writes=[pt])
                P.I("act", "activation", out=dend[:], in_=pt[:, 0:48], func=AF.Exp, reads=[pt], writes=[dend])

            for pc in (range(4) if (full or need_halo) else ()):
                wb = load_piece("pool%d" % pc, 256)
                for j in range(2):
                    blk = pc * 2 + j
                    pt = fm_block(wb, j, hb)
                    P.I("act", "copy", out=pbuf[:, blk, 16:144], in_=pt[:, 0:128], reads=[pt], writes=[pbuf])
            if full or need_halo:
                P.I("pool", "tensor_copy", out=pbuf[:, :, 0:16], in_=phalo[:], reads=[phalo], writes=[pbuf])
                P.I("pool", "tensor_copy", out=phalo[:], in_=pbuf[:, :, 128:144], reads=[pbuf], writes=[phalo])
            if full:
                for gi, w in enumerate((2, 4, 8, 16)):
                    src = pbuf[:, gi * 2:gi * 2 + 2, :]
                    cur_t = pbuf; cur = src
                    sh = 1
                    k = 0
                    while sh < w:
                        dstt = pw[k % 2]; k += 1
                        a_ap = cur
                        P.I("pool", "tensor_tensor", out=dstt[:, :, sh:144], in0=a_ap[:, :, sh:144],
                                                                                            in1=a_ap[:, :, 0:144 - sh], op=ALU.add,
                             reads=[cur_t], writes=[dstt])
                        cur_t = dstt; cur = dstt[:]
                        sh *= 2
                    P.I("dve", "scalar_tensor_tensor", out=pooled[:, gi * 2:gi * 2 + 2, :], in0=cur[:, :, 16:144],
                                                                                      scalar=1.0 / w, in1=pbuf[:, gi * 2:gi * 2 + 2, 16:144],
                                                                                      op0=ALU.mult, op1=ALU.subtract,
                         reads=[cur_t, pbuf], writes=[pooled])
                    for db in range(2):
                        pt = psF.next()
                        for cbk in range(2):
                            P.I("pe", "matmul", pt[:, 0:128], lhsT=poolw[:, gi * 2 + cbk, db * 128:(db + 1) * 128],
                                                                                        rhs=pooled[:, gi * 2 + cbk, :], start=(cbk == 0), stop=(cbk == 1),
                                 reads=[poolw, pooled], writes=[pt], inc=(cbk == 1))
                        P.I("act", "activation", out=mixT[:, gi * 2 + db, :], in_=pt[:, 0:128], func=AF.Identity,
                                                                                scale=pscale[:, gi * 2 + db:gi * 2 + db + 1],
                             reads=[pt, pscale], writes=[mixT])

            def grp(g):
                gp = g % 2
                hs = slice(g * 6, g * 6 + 6)
                wb = load_piece("x%d" % g, 384)
                ptx = psT.next()
                pt3 = psF.next()
                for j in range(3):
                    fm_block(wb, j, hb, pt=pt3, col=j * 128)
                for j in range(3):
                    xo = xsT[(g * 3 + j) % 3]
                    conv_block(pt3, g * 3 + j, is_meta, xo[:], xo, col=j * 128)
                for j in range(3):
                    xo = xsT[(g * 3 + j) % 3]
                    P.I("pe", "transpose", out=ptx[:, j * 128:(j + 1) * 128], in_=xo[:], identity=identF[:],
                         reads=[xo, identF], writes=[ptx], inc=(j == 2))
                xg = xs_g[gp]
                P.I("act", "copy", out=xg[:], in_=ptx[:, 0:384], reads=[ptx], writes=[xg])
                yield
                wb = load_piece("b%d" % g, 128)
                pt = fm_block(wb, 0, hb)
                conv_block(pt, 24 + g, is_meta, BT[gp][:], BT[gp])
                ptb = psT.next()
                ptb_bf = ptb[:].bitcast(BF16)
                P.I("pe", "transpose", out=ptb_bf[:, 0:128], in_=BT[gp][:], identity=identB[:],
                     reads=[BT[gp], identB], writes=[ptb])
                P.I("act", "copy", out=Btok[gp][:], in_=ptb_bf[:, 0:128], reads=[ptb], writes=[Btok[gp]])
                yield
                if full or need_halo:
                    wb = load_piece("c%d" % g, 128)
                    pt = fm_block(wb, 0, hb)
                    conv_block(pt, 32 + g, is_meta, CT[gp][:], CT[gp], skip_conv=not full)
                    yield

                rA = rhsA[gp]; L = Lt[gp]
                xd = Xdt[gp]; xe = Xdec[gp]
                P.I("pool", "tensor_tensor", out=xd[:].rearrange("p (a b) -> p a b", b=64),
                    in0=xg[:].rearrange("p (a b) -> p a b", b=64),
                    in1=bc(dtv[:, hs], 2, 64), op=ALU.mult,
                    reads=[xg, dtv], writes=[xd])
                if full:
                    P.I("pool", "tensor_tensor", out=rA[:], in0=bc(Adt[:, hs], 2, 128), in1=bc(triL[:], 1, 6), op=ALU.mult,
                        reads=[Adt, triL], writes=[rA])
                    P.I("pe", "matmul", psSeg[:, 0:512], lhsT=triU[:], rhs=rA[:, 0:4, :].rearrange("p a b -> p (a b)"), start=True, stop=True,
                        reads=[triU, rA], writes=[psSeg], inc=False)
                    P.I("pe", "matmul", psSeg[:, 512:768], lhsT=triU[:], rhs=rA[:, 4:6, :].rearrange("p a b -> p (a b)"), start=True, stop=True,
                        reads=[triU, rA], writes=[psSeg])
                    P.I("act", "activation", out=L[:].rearrange("p a b -> p (a b)"), in_=psSeg[:, 0:768], func=AF.Exp,
                        reads=[psSeg], writes=[L])
                    P.I("dve", "tensor_tensor", out=xe[:].rearrange("p (a b) -> p a b", b=64),
                        in0=xd[:].rearrange("p (a b) -> p a b", b=64),
                        in1=L[:, :, 127:128].to_broadcast([128, 6, 64]), op=ALU.mult,
                        reads=[xd, L], writes=[xe])
                else:
                    P.I("dve", "tensor_tensor", out=xe[:].rearrange("p (a b) -> p a b", b=64),
                        in0=xd[:].rearrange("p (a b) -> p a b", b=64),
                        in1=bc(dend[:, hs], 2, 64), op=ALU.mult,
                        reads=[xd, dend], writes=[xe])
                yield
                if full:
                    pcb = psT.next()
                    P.I("pe", "matmul", pcb[:, 0:128], lhsT=BT[gp][:], rhs=CT[gp][:], start=True, stop=True,
                         reads=[BT[gp], CT[gp]], writes=[pcb])
                    cm = CBm[gp]
                    P.I("dve", "tensor_tensor", out=cm[:], in0=pcb[:, 0:128], in1=triL[:], op=ALU.mult,
                         reads=[pcb, triL], writes=[cm])
                    mt = MT[gp]
                    P.I("dve", "tensor_tensor", out=mt[:], in0=L[:], in1=bc(cm[:], 1, 6), op=ALU.mult,
                         reads=[L, cm], writes=[mt])
                    pyd = psY.next()
                    for h in range(6):
                        P.I("pe", "matmul", pyd[:, h * 64:(h + 1) * 64], lhsT=mt[:, h, :], rhs=xd[:, h * 64:(h + 1) * 64],
                                                                                 start=True, stop=True,
                             reads=[mt, xd], writes=[pyd], inc=(h == 5))
                    pyo = psY.next()
                    P.I("pe", "matmul", pyo[:, 0:384], lhsT=CT[gp][:], rhs=stateB[g][:], start=True, stop=True,
                         reads=[CT[gp], stateB[g]], writes=[pyo])
                    ya = y1[gp]; yk = ysk[gp]
                    P.I("dve", "tensor_tensor", out=ya[:].rearrange("p (a b) -> p a b", b=64),
                                                                                in0=pyo[:, 0:384].rearrange("p (a b) -> p a b", b=64),
                                                                                in1=bc(eA[:, hs], 2, 64), op=ALU.mult,
                         reads=[pyo, eA], writes=[ya])
                    P.I("dve", "tensor_tensor", out=ya[:], in0=ya[:], in1=pyd[:, 0:384], op=ALU.add,
                         reads=[ya, pyd], writes=[ya])
                    P.I("pool", "tensor_tensor", out=yk[:].rearrange("p (a b) -> p a b", b=64),
                                                                               in0=xg[:].rearrange("p (a b) -> p a b", b=64),
                                                                               in1=bc(Dsk[:, hs], 2, 64), op=ALU.mult,
                         reads=[xg, Dsk], writes=[yk])
                    P.I("pool", "tensor_tensor", out=ya[:], in0=ya[:], in1=yk[:], op=ALU.add, reads=[ya, yk], writes=[ya])
                    yield
                pns = psY.next()
                P.I("pe", "matmul", pns[:, 0:384], lhsT=Btok[gp][:], rhs=xe[:], start=True, stop=True,
                     reads=[Btok[gp], xe], writes=[pns])
                sg = state[g]
                P.I("pool", "tensor_tensor", out=sg[:].rearrange("p (a b) -> p a b", b=64),
                                                                    in0=sg[:].rearrange("p (a b) -> p a b", b=64),
                                                                    in1=bc(dtot[:, hs], 2, 64), op=ALU.mult,
                     reads=[sg, dtot], writes=[sg])
                P.I("dve", "tensor_tensor", out=sg[:], in0=sg[:], in1=pns[:, 0:384], op=ALU.add, reads=[sg, pns], writes=[sg])
                P.I("act", "copy", out=stateB[g][:], in_=sg[:], reads=[sg], writes=[stateB[g]])
                if not full:
                    return
                yield
                wb = load_piece("z%d" % g, 384)
                pz = psF.next()
                for kc in range(NKC):
                    P.I("pe", "matmul", pz[:, 0:384], lhsT=hb[:, kc, :], rhs=wb[:, kc, 0:384], start=(kc == 0), stop=(kc == NKC - 1),
                         reads=[hb, wb], writes=[pz], inc=(kc == NKC - 1))
                zz = zs[gp]
                P.I("act", "activation", out=zz[:], in_=pz[:, 0:384], func=AF.Silu, reads=[pz], writes=[zz])
                P.I("dve", "tensor_tensor", out=ya[:], in0=ya[:], in1=zz[:], op=ALU.mult, reads=[ya, zz], writes=[ya])
                sq = ssq[gp]
                P.I("pool", "memset", sq[:], 0.0, writes=[sq])
                P.I("act", "activation", out=junk[:], in_=ya[:], func=AF.Square, accum_out=sq[:], reads=[ya], writes=[junk, sq])
                P.I("act", "activation", out=sq[:], in_=sq[:], func=AF.Sqrt, bias=epsc[:, 0:1], scale=1.0 / 384.0,
                     reads=[sq, epsc], writes=[sq])
                P.I("dve", "reciprocal", out=sq[:], in_=sq[:], reads=[sq], writes=[sq])
                yy = yn[gp]
                P.I("dve", "tensor_scalar_mul", out=yy[:], in0=ya[:], scalar1=sq[:, 0:1], reads=[ya, sq], writes=[yy])
                pty = psT.next()
                pty_bf = pty[:].bitcast(BF16)
                for j in range(3):
                    P.I("pe", "transpose", out=pty_bf[:, j * 128:(j + 1) * 128], in_=yy[:, j * 128:(j + 1) * 128],
                                                                                identity=identB[:],
                         reads=[yy, identB], writes=[pty], inc=(j == 2))
                for j in range(3):
                    blk = g * 3 + j
                    P.I("act", "activation", out=mixT[:, 8 + blk, :], in_=pty_bf[:, j * 128:(j + 1) * 128], func=AF.Identity,
                                                                                   scale=normg[:, blk:blk + 1],
                         reads=[pty, normg], writes=[mixT])

            act_g = []
            nxt_g = 0
            stag = 3 if full else 2
            while nxt_g < 8 or act_g:
                if nxt_g < 8 and (not act_g or (len(act_g) < 2 and act_g[-1][1] >= stag)):
                    act_g.append([grp(nxt_g), 0]); nxt_g += 1
                for a_ in list(act_g):
                    try:
                        next(a_[0]); a_[1] += 1
                    except StopIteration:
                        act_g.remove(a_)
            if not full:
                return
            accW = [psF.tiles[0], psF.tiles[1], psY.tiles[0], psY.tiles[1]]
            for pcw in range(16):
                wo = wobuf[woi[0] % 3]; woi[0] += 1
                P.D("sp", dict(out=wo[:], in_=w2s[pcw].t), "wobuf%d" % ((woi[0] - 1) % 3), reads=[w2s[pcw]], writes=[wo])
                for kk in range(2):
                    kc = pcw * 2 + kk
                    for db in range(4):
                        P.I("pe", "matmul", accW[db][:], lhsT=mixT[:, kc, :], rhs=wo[:, kk, db * 512:(db + 1) * 512], start=(kc == 0), stop=(kc == 31),
                            reads=[mixT, wo], writes=[accW[db]], inc=(kc == 31 or (kk == 1 and db == 3)))
            for db in range(4):
                if db % 2 == 0:
                    P.I("act", "copy", out=xn[:, db * 512:(db + 1) * 512], in_=accW[db][:], reads=[accW[db]], writes=[xn])
                else:
                    P.I("dve", "tensor_copy", out=xn[:, db * 512:(db + 1) * 512], in_=accW[db][:], reads=[accW[db]], writes=[xn])
            for g4 in range(4):
                po = psT.next()
                for j in range(4):
                    dblk = g4 * 4 + j
                    P.I("pe", "transpose", out=po[:, j * 128:(j + 1) * 128], in_=xn[:, dblk * 128:(dblk + 1) * 128], identity=identF[:],
                        reads=[xn, identF], writes=[po], inc=(j == 3))
                P.I("dve", "scalar_tensor_tensor", out=vT[:, g4 * 4:(g4 + 1) * 4, :].rearrange("p a b -> p (a b)"),
                    in0=h0[:, g4 * 4:(g4 + 1) * 4, :].rearrange("p a b -> p (a b)"), scalar=ALPHA, in1=po[:],
                    op0=ALU.mult, op1=ALU.add, reads=[h0, po], writes=[vT])
            P.I("act", "activation", out=xn16, in_=vT[:], func=AF.Square, reads=[vT], writes=[xn])
            pm = psT.next()
            for kc in range(NKC):
                P.I("pe", "matmul", pm[:, 0:128], lhsT=ones[:], rhs=vT[:, kc, :], start=(kc == 0), stop=(kc == NKC - 1),
                     reads=[ones, vT], writes=[pm], inc=(kc == NKC - 1))
            pq = psT.next()
            for kc in range(NKC):
                P.I("pe", "matmul", pq[:, 0:128], lhsT=ones[:], rhs=xn16[:, kc, :], start=(kc == 0), stop=(kc == NKC - 1),
                     reads=[ones, xn], writes=[pq], inc=(kc == NKC - 1))
            P.I("act", "mul", out=lmean[:], in_=pm[:, 0:128], mul=1.0 / D, reads=[pm], writes=[lmean])
            P.I("act", "mul", out=lex2[:], in_=pq[:, 0:128], mul=1.0 / D, reads=[pq], writes=[lex2])
            P.I("dve", "tensor_tensor", out=lrstd[:], in0=lmean[:], in1=lmean[:], op=ALU.mult, reads=[lmean], writes=[lrstd])
            P.I("dve", "tensor_tensor", out=lrstd[:], in0=lex2[:], in1=lrstd[:], op=ALU.subtract, reads=[lex2, lrstd], writes=[lrstd])
            P.I("act", "activation", out=lrstd[:], in_=lrstd[:], func=AF.Sqrt, bias=epsc[:, 0:1], scale=1.0, reads=[lrstd, epsc], writes=[lrstd])
            P.I("dve", "reciprocal", out=lrstd[:], in_=lrstd[:], reads=[lrstd], writes=[lrstd])
            P.I("dve", "tensor_tensor", out=vT[:], in0=vT[:], in1=bc(lmean[:], 1, NKC), op=ALU.subtract, reads=[vT, lmean], writes=[vT])
            P.I("dve", "tensor_tensor", out=vT[:], in0=vT[:], in1=bc(lrstd[:], 1, NKC), op=ALU.mult, reads=[vT, lrstd], writes=[vT])
            for kc in range(NKC):
                P.I("act", "activation", out=h1T[:, kc, :], in_=vT[:, kc, :], func=AF.Identity, bias=b1[:, kc:kc + 1], scale=g1[:, kc:kc + 1],
                     reads=[vT, b1, g1], writes=[h1T])
            P.D("sp", dict(out=h1s.t[xi], in_=h1T[:]), "h1T", reads=[h1T], writes=[h1s])
            if debug:
                P.D("sp", dict(out=dbg["h1T"][xi], in_=h1T[:]), "dbg", reads=[h1T])
                P.D("sp", dict(out=dbg["mixT"][xi], in_=mixT[:]), "dbg", reads=[mixT])

        tile(0, meta_in, True, False)
        for i in range(NPRE):
            tile(1 + i, xpre_in[i * 128:(i + 1) * 128, :], False, False, prefix=True, need_halo=(i == NPRE - 1))
        for i in range(NT):
            tile(1 + NPRE + i, x_in[i * 128:(i + 1) * 128, :], False, True, xi=i)

        P.barrier()
        esA.close()
        if do_peer:
            Lb = dict(locals())
            phase_b(Lb)
            esB = Lb["_esB2"]
            P.final_wait("sp", [out_tk])
        else:
            P.final_wait("sp", [h1s])
        P.emit(block)
        if do_peer:
            esB.close()
        print("ops:", P.nops, "sems:", P.nsem, {e: len(P.ops[e]) for e in P.ENG})
    return nc


U32 = mybir.dt.uint32
NEG = -1.0e30
TBS = 4


def phase_b(L):
    nc, P, sb, ps, NT = L["nc"], L["P"], L["sb"], L["ps"], L["NT"]
    h1s, wqs, out_tk, out = L["h1s"], L["wqs"], L["out_tk"], L["out"]
    dq = L["dq"]; cur_es = L["cur_es"]
    uts2 = [Tk(nc.dram_tensor("uts2_%d" % j, [128, 2, NKC, 128], BF16).ap(), "uts2_%d" % j) for j in range(64)]
    vss2 = [Tk(nc.dram_tensor("vss2_%d" % j, [128, 2, D], BF16).ap(), "vss2_%d" % j) for j in range(64)]
    gts = Tk(nc.dram_tensor("gts", [32, NT, 128, 4, 128], BF16).ap(), "gts")

    es0 = ExitStack(); cur_es[0] = es0
    identF = sb("b0_identF", [128, 128])
    pf = [sb("b0_pf%d" % i, [128, D]) for i in range(3)]
    pbf = [sb("b0_pbf%d" % i, [128, D], BF16) for i in range(3)]
    ubf = [sb("b0_ubf%d" % i, [128, NKC, 128], BF16) for i in range(3)]
    psT0 = Rot([ps("b0_psT%d" % i, [128, 512]) for i in range(4)])
    P.D("sp", dict(out=identF[:], in_=L["ident_in"]), "cB", writes=[identF])
    pi = [0]

    def convert(src_ap, dsts):
        i = pi[0]; pi[0] += 1
        f = pf[i % 3]; btk = pbf[i % 3]
        q = dq.next()
        P.D(q, dict(out=f[:], in_=src_ap), "pB_f%d" % (i % 3), writes=[f])
        if i % 2 == 0:
            P.I("act", "copy", out=btk[:], in_=f[:], reads=[f], writes=[btk])
        else:
            P.I("dve", "tensor_copy", out=btk[:], in_=f[:], reads=[f], writes=[btk])
        for (dtk, dap, c0, w) in dsts:
            P.D(q, dict(out=dap, in_=btk[:, c0:c0 + w]), "pB_b%d" % (i % 3), reads=[btk], writes=[dtk])

    for kc in range(NKC):
        convert(L["peer_wq"][kc * 128:(kc + 1) * 128, :], [(wqs[j], wqs[j].t[:, kc, :], j * 128, 128) for j in range(16)])
    for k1 in range(128):
        convert(L["peer_v"][k1 * 128:(k1 + 1) * 128, :], [(vss2[k1 // 2], vss2[k1 // 2].t[:, k1 % 2, :], 0, D)])
    for k1 in range(128):
        i = pi[0]; pi[0] += 1
        f = pf[i % 3]; ub = ubf[i % 3]
        q = dq.next()
        P.D(q, dict(out=f[:], in_=L["peer_u"][k1 * 128:(k1 + 1) * 128, :]), "pB_f%d" % (i % 3), writes=[f])
        for g4 in range(4):
            pt = psT0.next()
            for j in range(4):
                kc = g4 * 4 + j
                P.I("pe", "transpose", out=pt[:, j * 128:(j + 1) * 128], in_=f[:, kc * 128:(kc + 1) * 128], identity=identF[:],
                    reads=[f, identF], writes=[pt], inc=(j == 3))
            if g4 % 2 == 0:
                P.I("act", "copy", out=ub[:, g4 * 4:(g4 + 1) * 4, :].rearrange("p a b -> p (a b)"), in_=pt[:], reads=[pt], writes=[ub])
            else:
                P.I("dve", "tensor_copy", out=ub[:, g4 * 4:(g4 + 1) * 4, :].rearrange("p a b -> p (a b)"), in_=pt[:], reads=[pt], writes=[ub])
        P.D(q, dict(out=uts2[k1 // 2].t[:, k1 % 2], in_=ub[:]), "pB_u%d" % (i % 3), reads=[ub], writes=[uts2[k1 // 2]])
    P.barrier()
    es0.close()

    es1 = ExitStack(); cur_es[0] = es1
    identF = sb("b1_identF", [128, 128]); identB = sb("b1_identB", [128, 128], BF16)
    iota = sb("b1_iota", [128, 128]); keysT = sb("b1_keysT", [128, 2, 128])
    NB = 2

    def dbl(n, shp, dt=F32):
        return [sb("b1_%s%d" % (n, i), shp, dt) for i in range(NB)]

    h1T_ = dbl("h1T", [128, NKC, 128]); h1Tb_ = dbl("h1Tb", [128, NKC, 128], BF16)
    qTt_ = dbl("qT", [128, NKC, 128])
    s_sb_ = dbl("s_sb", [128, 16, 128]); swork_ = dbl("swork", [128, 128])
    a_all_ = dbl("a_all", [128, 16, 16]); idx_ = dbl("idx", [128, 8, 16], U32); idxf_ = dbl("idxf", [128, 128])
    cand_ = dbl("cand", [128, 256]); cw1_ = dbl("cw1", [128, 256]); cw2_ = dbl("cw2", [128, 256])
    ct_ = dbl("ct", [128, 8, 24]); tau_ = dbl("tau", [128, 8]); zsum_ = dbl("zsum", [128, 8]); cex_ = dbl("cex", [128, 8, 16])
    thr_ = dbl("thr", [128, 8, 16]); Fa_ = dbl("Fa", [128, 8, 16]); E2_ = dbl("E2", [128, 8, 128])
    FaT_ = dbl("FaT", [128, 128]); idxT_ = dbl("idxT", [128, 128])
    Rs_ = dbl("Rs", [128, 8, 16, 16], BF16); Rm_ = [sb("b1_Rm", [128, 8, 16, 16], BF16)] * NB
    RT_ = dbl("RT", [128, 128, 128], BF16)
    P1T_ = [sb("b1_P1T", [128, 16, 128], BF16)] * NB
    GT_ = [sb("b1_GT", [128, 128, 128], BF16)] * NB
    wqb = [sb("b1_wqb%d" % i, [128, NKC, 128], BF16) for i in range(2)]
    psT_ = [Rot([ps("b1_psT%d_%d" % (b, i), [128, 512]) for i in range(4)]) for b in range(NB)]

    c = "cB"
    P.D("sp", dict(out=identF[:], in_=L["ident_in"]), c, writes=[identF])
    P.D("sp", dict(out=iota[:], in_=L["iota_in"]), c, writes=[iota])
    P.D("sp", dict(out=keysT[:], in_=L["keysT_in"]), c, writes=[keysT])
    P.I("dve", "tensor_copy", out=identB[:], in_=identF[:], reads=[identF], writes=[identB])
    wqi = [0]

    def tileB1(ti):
        b = ti % NB
        h1T = h1T_[b]; h1Tb = h1Tb_[b]; qTt = qTt_[b]; qT = qTt.t; s_sb = s_sb_[b]; swork = swork_[b]
        a_all = a_all_[b]; idx = idx_[b]; idxf = idxf_[b]; ct = ct_[b]; tau = tau_[b]; zsum = zsum_[b]; cex = cex_[b]
        thr = thr_[b]; Fa = Fa_[b]; E2 = E2_[b]; FaT = FaT_[b]; idxT = idxT_[b]; RT = RT_[b]; GT = GT_[b]
        psT = psT_[b]
        P.D("sp", dict(out=h1T[:], in_=h1s.t[ti]), "b1_h1T%d" % b, reads=[h1s], writes=[h1T])
        P.I("pool", "tensor_copy", out=h1Tb[:], in_=h1T[:], reads=[h1T], writes=[h1Tb])
        yield
        for g4 in range(4):
            pq = psT.next()
            for j in range(4):
                qb = g4 * 4 + j
                w = wqb[wqi[0] % 2]; slot = wqi[0] % 2; wqi[0] += 1
                P.D("sp", dict(out=w[:], in_=wqs[qb].t), "b1_wq%d" % slot, reads=[wqs[qb]], writes=[w])
                for kc in range(NKC):
                    P.I("pe", "matmul", pq[:, j * 128:(j + 1) * 128], lhsT=w[:, kc, :], rhs=h1Tb[:, kc, :], start=(kc == 0), stop=(kc == NKC - 1),
                        reads=[w, h1Tb], writes=[pq], inc=(kc == NKC - 1))
                yield
            P.I("act", "copy", out=qT[:, g4 * 4:(g4 + 1) * 4, :].rearrange("p a b -> p (a b)"), in_=pq[:], reads=[pq], writes=[qTt])
        for g4 in range(4):
            pss = psT.next()
            for j in range(4):
                seg = g4 * 4 + j
                P.I("pe", "matmul", pss[:, j * 128:(j + 1) * 128], lhsT=qT[:, seg, :], rhs=keysT[:, seg % 2, :], start=True, stop=True,
                    reads=[qTt, keysT], writes=[pss], inc=(j == 3))
            P.I("act", "copy", out=s_sb[:, g4 * 4:(g4 + 1) * 4, :].rearrange("p a b -> p (a b)"), in_=pss[:], reads=[pss], writes=[s_sb])
            yield
        for seg in range(16):
            h = seg // 2
            P.I("dve", "max", out=a_all[:, seg, 0:8], in_=s_sb[:, seg, :], reads=[s_sb], writes=[a_all])
            if seg % 2 == 0:
                P.I("dve", "max_index", out=idx[:, h, 0:8], in_max=a_all[:, seg, 0:8], in_values=s_sb[:, seg, :], reads=[a_all, s_sb], writes=[idx])
            P.I("dve", "match_replace", out=swork[:], in_to_replace=a_all[:, seg, 0:8], in_values=s_sb[:, seg, :], imm_value=NEG,
                reads=[a_all, s_sb], writes=[swork])
            P.I("dve", "max", out=a_all[:, seg, 8:16], in_=swork[:], reads=[swork], writes=[a_all])
            if seg % 2 == 0:
                P.I("dve", "max_index", out=idx[:, h, 8:16], in_max=a_all[:, seg, 8:16], in_values=swork[:], reads=[a_all, swork], writes=[idx])
            yield
        for h in range(8):
            cd = cand_[b]; c1 = cw1_[b]; c2 = cw2_[b]
            P.I("dve", "tensor_tensor", out=cd[:].rearrange("p (a b) -> p a b", b=16), in0=bc(a_all[:, 2 * h, :], 2, 16), in1=bc(a_all[:, 2 * h + 1, :], 1, 16),
                op=ALU.add, reads=[a_all], writes=[cd])
            P.I("dve", "max", out=ct[:, h, 0:8], in_=cd[:], reads=[cd], writes=[ct])
            P.I("dve", "match_replace", out=c1[:], in_to_replace=ct[:, h, 0:8], in_values=cd[:], imm_value=NEG, reads=[ct, cd], writes=[c1])
            P.I("dve", "max", out=ct[:, h, 8:16], in_=c1[:], reads=[c1], writes=[ct])
            P.I("dve", "match_replace", out=c2[:], in_to_replace=ct[:, h, 8:16], in_values=c1[:], imm_value=NEG, reads=[ct, c1], writes=[c2])
            P.I("dve", "max", out=ct[:, h, 16:24], in_=c2[:], reads=[c2], writes=[ct])
            yield
        P.I("dve", "tensor_tensor", out=tau[:], in0=ct[:, :, 15], in1=ct[:, :, 16], op=ALU.add, reads=[ct], writes=[tau])
        P.I("dve", "tensor_scalar_mul", out=tau[:], in0=tau[:], scalar1=0.5, reads=[tau], writes=[tau])
        P.I("dve", "tensor_tensor", out=cex[:], in0=ct[:, :, 0:16], in1=ct[:, :, 0:1].to_broadcast([128, 8, 16]), op=ALU.subtract, reads=[ct], writes=[cex])
        P.I("act", "activation", out=cex[:], in_=cex[:], func=AF.Exp, reads=[cex], writes=[cex])
        P.I("dve", "tensor_reduce", out=zsum[:], in_=cex[:], axis=mybir.AxisListType.X, op=ALU.add, reads=[cex], writes=[zsum])
        P.I("dve", "reciprocal", out=zsum[:], in_=zsum[:], reads=[zsum], writes=[zsum])
        yield
        a1 = a_all[:].rearrange("p (h i) j -> p h i j", i=2)[:, :, 0, :]
        a2 = a_all[:].rearrange("p (h i) j -> p h i j", i=2)[:, :, 1, :]
        P.I("dve", "tensor_tensor", out=thr[:], in0=bc(tau[:], 2, 16), in1=a1, op=ALU.subtract, reads=[tau, a_all], writes=[thr])
        P.I("dve", "tensor_tensor", out=Fa[:], in0=a1, in1=a1[:, :, 0:1].to_broadcast([128, 8, 16]), op=ALU.subtract, reads=[a_all], writes=[Fa])
        P.I("act", "activation", out=Fa[:], in_=Fa[:], func=AF.Exp, reads=[Fa], writes=[Fa])
        P.I("dve", "tensor_tensor", out=Fa[:], in0=Fa[:], in1=bc(zsum[:], 2, 16), op=ALU.mult, reads=[Fa, zsum], writes=[Fa])
        s2 = s_sb[:].rearrange("p (h i) k -> p h i k", i=2)[:, :, 1, :]
        P.I("dve", "tensor_tensor", out=E2[:], in0=s2, in1=a2[:, :, 0:1].to_broadcast([128, 8, 128]), op=ALU.subtract, reads=[s_sb, a_all], writes=[E2])
        P.I("act", "activation", out=E2[:], in_=E2[:], func=AF.Exp, reads=[E2], writes=[E2])
        P.I("dve", "tensor_copy", out=idxf[:].rearrange("p (h j) -> p h j", j=16), in_=idx[:], reads=[idx], writes=[idxf])
        yield
        ptf = psT.next()
        P.I("pe", "transpose", out=ptf[:, 0:128], in_=Fa[:].rearrange("p h j -> p (h j)"), identity=identF[:], reads=[Fa, identF], writes=[ptf], inc=False)
        P.I("pe", "transpose", out=ptf[:, 128:256], in_=idxf[:], identity=identF[:], reads=[idxf, identF], writes=[ptf])
        P.I("act", "copy", out=FaT[:], in_=ptf[:, 0:128], reads=[ptf], writes=[FaT])
        P.I("act", "copy", out=idxT[:], in_=ptf[:, 128:256], reads=[ptf], writes=[idxT])
        yield
        for ks in range(8):
            rm = Rm_[b]; rs = Rs_[b]
            ksl = slice(ks * 16, (ks + 1) * 16)
            P.I("dve", "tensor_tensor", out=rm[:], in0=s2[:, :, ksl].unsqueeze(2).to_broadcast([128, 8, 16, 16]),
                in1=thr[:].unsqueeze(3).to_broadcast([128, 8, 16, 16]), op=ALU.is_ge, reads=[s_sb, thr], writes=[rm])
            P.I("pool", "tensor_tensor", out=rs[:], in0=rm[:], in1=E2[:, :, ksl].unsqueeze(2).to_broadcast([128, 8, 16, 16]), op=ALU.mult,
                reads=[rm, E2], writes=[rs])
            for half in range(2):
                pr = psT.next()
                prb = pr[:].bitcast(BF16)
                for j in range(8):
                    kk = half * 8 + j
                    P.I("pe", "transpose", out=prb[:, j * 128:(j + 1) * 128], in_=rs[:, :, :, kk].rearrange("p h j -> p (h j)"), identity=identB[:],
                        reads=[rs, identB], writes=[pr], inc=(j == 7))
                k0 = ks * 16 + half * 8
                P.I("dve", "tensor_tensor", out=RT[:, :, k0:k0 + 8].rearrange("p t k -> p k t"), in0=prb[:, 0:1024].rearrange("p (k t) -> p k t", t=128),
                    in1=bc(FaT[:], 1, 8), op=ALU.mult, reads=[pr, FaT], writes=[RT])
                yield
        for ts in range(8):
            p1 = P1T_[b]
            P.I("dve", "tensor_tensor", out=p1[:], in0=bc(iota[:], 1, 16), in1=bc(idxT[:, ts * 16:(ts + 1) * 16], 2, 128), op=ALU.is_equal,
                reads=[iota, idxT], writes=[p1])
            for t4 in range(4):
                pg = psT.next()
                for j in range(4):
                    tl = t4 * 4 + j
                    t = ts * 16 + tl
                    P.I("pe", "matmul", pg[:, j * 128:(j + 1) * 128], lhsT=RT[:, t, :], rhs=p1[:, tl, :], start=True, stop=True,
                        reads=[RT, p1], writes=[pg], inc=(j == 3))
                t0 = ts * 16 + t4 * 4
                P.I("act", "copy", out=GT[:, :, t0:t0 + 4].rearrange("p k t -> p t k"), in_=pg[:].rearrange("p (t k) -> p t k", k=128),
                    reads=[pg], writes=[GT])
                yield
        P.D("sp", dict(out=gts.t[:, ti].rearrange("c p j t -> p c (j t)"), in_=GT[:].rearrange("p (c j) t -> p c (j t)", j=4)), "b1_gt",
            reads=[GT], writes=[gts])
        yield

    HALF = 50
    active = []
    nxt = 0
    while nxt < NT or active:
        if nxt < NT and (not active or (len(active) < NB and active[-1][1] >= HALF)):
            active.append([tileB1(nxt), 0]); nxt += 1
        for a in list(active):
            try:
                next(a[0]); a[1] += 1
            except StopIteration:
                active.remove(a)
    P.barrier()
    es1.close()

    es2 = ExitStack(); cur_es[0] = es2
    TB = TBS * 128
    assert NT % TBS == 0 or NT < TBS
    tbs = min(TBS, NT); TBt = tbs * 128
    identF = sb("b2_identF", [128, 128])
    g2bc = sb("b2_g2bc", [128, D]); b2bc = sb("b2_b2bc", [128, D]); epsc = sb("b2_epsc", [128, 1])
    hst = [sb("b2_hst%d" % i, [128, NKC, 128]) for i in range(2)]
    h1Tb = sb("b2_h1Tb", [128, NKC, TBt], BF16)
    NUB = 4
    ub2 = [sb("b2_ub%d" % i, [128, 2, NKC, 128], BF16) for i in range(NUB)]
    vb2 = [sb("b2_vb%d" % i, [128, 2, D], BF16) for i in range(NUB)]
    gtb = [sb("b2_gtb%d" % i, [128, 4, TBt], BF16) for i in range(3)]
    gS = [sb("b2_gS%d" % i, [128, TBt], BF16) for i in range(4)]
    AT = [sb("b2_AT%d" % i, [128, 4, TBt], BF16) for i in range(2)]
    osb = [sb("b2_osb%d" % i, [128, D]) for i in range(tbs)]
    st6 = sb("b2_st6", [128, 4, 6]); mv = sb("b2_mv", [128, 2]); rstd = sb("b2_rstd", [128, 1])
    pS = [ps("b2_pS%d" % i, [128, 512]) for i in range(4)]
    acc = [ps("b2_acc%d" % i, [128, 512]) for i in range(4)]
    P.D("sp", dict(out=identF[:], in_=L["ident_in"]), c, writes=[identF])
    P.D("sp", dict(out=g2bc[:], in_=L["ln2_g"].partition_broadcast(128)), c, writes=[g2bc])
    P.D("sp", dict(out=b2bc[:], in_=L["ln2_b"].partition_broadcast(128)), c, writes=[b2bc])
    P.I("dve", "memset", epsc[:], LN_EPS, writes=[epsc])
    ui = [0]; gi = [0]

    def macro(mi):
        t_lo = mi * tbs
        for st in range(tbs):
            hs_ = hst[st % 2]
            P.D("sp", dict(out=hs_[:], in_=h1s.t[t_lo + st]), "b2_hst%d" % (st % 2), reads=[h1s], writes=[hs_])
            P.I("pool", "tensor_copy", out=h1Tb[:, :, st * 128:(st + 1) * 128], in_=hs_[:], reads=[hs_], writes=[h1Tb])
            P.I("pool", "memset", osb[st][:], 0.0, writes=[osb[st]])
        for cg in range(32):
            gb = gtb[gi[0] % 3]; gslot = gi[0] % 3; gi[0] += 1
            P.D("sp", dict(out=gb[:].rearrange("p j (s t) -> p j s t", t=128),
                           in_=gts.t[cg, t_lo:t_lo + tbs].rearrange("s p j t -> p j s t")), "b2_gt%d" % gslot, reads=[gts], writes=[gb])
            a_ = AT[cg % 2]
            vpair = []
            for pr_ in range(2):
                pair = cg * 2 + pr_
                u = ub2[ui[0] % NUB]; v = vb2[ui[0] % NUB]; slot = ui[0] % NUB; ui[0] += 1
                P.D("sp", dict(out=u[:], in_=uts2[pair].t), "b2_u%d" % slot, reads=[uts2[pair]], writes=[u])
                P.D("sp", dict(out=v[:], in_=vss2[pair].t), "b2_v%d" % slot, reads=[vss2[pair]], writes=[v])
                vpair.append(v)
                for jj in range(2):
                    j = pr_ * 2 + jj
                    for kc in range(NKC):
                        P.I("pe", "matmul", pS[j][:, 0:TBt], lhsT=u[:, jj, kc, :], rhs=h1Tb[:, kc, :], start=(kc == 0), stop=(kc == NKC - 1),
                            reads=[u, h1Tb], writes=[pS[j]], inc=(kc == NKC - 1))
                    g_ = gS[j]
                    P.I("act", "activation", out=g_[:], in_=pS[j][:, 0:TBt], func=AF.Gelu, reads=[pS[j]], writes=[g_])
                    P.I("pool", "tensor_tensor", out=a_[:, j, :], in0=g_[:], in1=gb[:, j, :], op=ALU.mult, reads=[g_, gb], writes=[a_])
            for st in range(tbs):
                for db in range(4):
                    for j in range(4):
                        P.I("pe", "matmul", acc[db][:], lhsT=a_[:, j, st * 128:(st + 1) * 128], rhs=vpair[j // 2][:, j % 2, db * 512:(db + 1) * 512],
                            start=(j == 0), stop=(j == 3), reads=[a_, vpair[j // 2]], writes=[acc[db]], inc=(j == 3))
                    P.I("dve", "tensor_tensor", out=osb[st][:, db * 512:(db + 1) * 512], in0=osb[st][:, db * 512:(db + 1) * 512], in1=acc[db][:], op=ALU.add,
                        reads=[osb[st], acc[db]], writes=[osb[st]])
        for st in range(tbs):
            o = osb[st]; hs_ = hst[st % 2]
            P.D("sp", dict(out=hs_[:], in_=h1s.t[t_lo + st]), "b2_hst%d" % (st % 2), reads=[h1s], writes=[hs_])
            for g4 in range(4):
                pt = pS[g4]
                for j in range(4):
                    kc = g4 * 4 + j
                    P.I("pe", "transpose", out=pt[:, j * 128:(j + 1) * 128], in_=hs_[:, kc, :], identity=identF[:], reads=[hs_, identF], writes=[pt], inc=(j == 3))
                P.I("dve", "scalar_tensor_tensor", out=o[:, g4 * 512:(g4 + 1) * 512], in0=pt[:], scalar=ALPHA, in1=o[:, g4 * 512:(g4 + 1) * 512],
                    op0=ALU.mult, op1=ALU.add, reads=[pt, o], writes=[o])
            for c4 in range(4):
                P.I("dve", "bn_stats", out=st6[:, c4, :], in_=o[:, c4 * 512:(c4 + 1) * 512], reads=[o], writes=[st6])
            P.I("dve", "bn_aggr", out=mv[:], in_=st6[:].rearrange("p a b -> p (a b)"), reads=[st6], writes=[mv])
            P.I("act", "activation", out=rstd[:], in_=mv[:, 1:2], func=AF.Sqrt, bias=epsc[:, 0:1], scale=1.0, reads=[mv, epsc], writes=[rstd])
            P.I("dve", "reciprocal", out=rstd[:], in_=rstd[:], reads=[rstd], writes=[rstd])
            P.I("dve", "tensor_scalar", out=o[:], in0=o[:], scalar1=mv[:, 0:1], scalar2=rstd[:, 0:1], op0=ALU.subtract, op1=ALU.mult,
                reads=[o, mv, rstd], writes=[o])
            P.I("pool", "tensor_tensor", out=o[:], in0=o[:], in1=g2bc[:], op=ALU.mult, reads=[o, g2bc], writes=[o])
            P.I("pool", "tensor_tensor", out=o[:], in0=o[:], in1=b2bc[:], op=ALU.add, reads=[o, b2bc], writes=[o])
            ti = t_lo + st
            P.D("sp", dict(out=out[ti * 128:(ti + 1) * 128, :], in_=o[:]), "b2_out", reads=[o], writes=[out_tk])

    import os as _os
    for mi in range(0 if _os.environ.get("KB_SKIP_B2") else NT // tbs):
        macro(mi)
    L["_esB2"] = es2


def host_consts():
    k = np.arange(128)
    return {
        "ident": np.eye(128, dtype=np.float32),
        "tril": (k[:, None] <= k[None, :]).astype(np.float32),
        "triu": (k[:, None] > k[None, :]).astype(np.float32),
        "mmask": (k >= 112).astype(np.float32).reshape(128, 1),
        "iota": np.tile(k.astype(np.float32)[None, :], (128, 1)),
    }


def core_inputs(inp, b, t0, NT, NPRE=0):
    f = np.float32
    c = host_consts()
    meta = np.zeros((128, D), f)
    meta[112:] = inp["meta_tokens"]
    if NPRE == 0:
        xpre = np.zeros((128, D), f); flag = 0.0
    elif t0 == 0:
        xpre = np.ascontiguousarray(np.tile(meta, (NPRE, 1))); flag = 0.0
    else:
        assert t0 == NPRE
        xpre = np.ascontiguousarray(inp["x"][b, 0:t0 * 128]); flag = 1.0
    m = {
        "xpre": xpre,
        "flag": np.full((128, 1), flag, f),
        "x": np.ascontiguousarray(inp["x"][b, t0 * 128:(t0 + NT) * 128]),
        "meta": meta,
        "ln_in_g": np.ascontiguousarray(inp["ln_in_g"].reshape(NKC, 128).T),
        "ln_in_b": np.ascontiguousarray(inp["ln_in_b"].reshape(NKC, 128).T),
        "w_in": np.ascontiguousarray(inp["w_in"][0]),
        "pool_w": np.ascontiguousarray(inp["pool_w"][0]),
        "pool_scale": np.ascontiguousarray(inp["pool_scale"][0].reshape(8, 128).T),
        "conv_w": np.ascontiguousarray(inp["conv_w"][0].reshape(4, 40, 128).transpose(2, 1, 0)),
        "conv_b": np.ascontiguousarray(inp["conv_b"][0].reshape(40, 128).T),
        "dt_bias": np.ascontiguousarray(inp["dt_bias"][0]),
        "A_log": np.ascontiguousarray(inp["A_log"][0]),
        "D_skip": np.ascontiguousarray(inp["D_skip"][0]),
        "ssd_norm_g": np.ascontiguousarray(inp["ssd_norm_g"][0].reshape(24, 128).T),
        "w_out": np.ascontiguousarray(inp["w_out"][0]),
        "ln1_g": np.ascontiguousarray(inp["ln1_g"][0].reshape(NKC, 128).T),
        "ln1_b": np.ascontiguousarray(inp["ln1_b"][0].reshape(NKC, 128).T),
        "peer_wq": np.ascontiguousarray(inp["peer_wq"][0]),
        "keysT": np.ascontiguousarray(inp["peer_keys"][0].transpose(2, 0, 1)),
        "peer_u": np.ascontiguousarray(inp["peer_u"][0]),
        "peer_v": np.ascontiguousarray(inp["peer_v"][0]),
        "ln2_g": np.ascontiguousarray(inp["ln2_g"][0]),
        "ln2_b": np.ascontiguousarray(inp["ln2_b"][0]),
    }
    m.update(c)
    return m


N_CORES = 8
NT_CORE = 32
_NC_CACHE = {}


def kernel(**inputs):
    inp = {k: np.asarray(v) for k, v in inputs.items()}
    B, S, _ = inp["x"].shape
    halves = N_CORES // B
    assert halves == 2 and S == 2 * NT_CORE * 128
    if "nc" not in _NC_CACHE:
        _NC_CACHE["nc"] = build(NT_CORE, NPRE=NT_CORE)
    nc = _NC_CACHE["nc"]
    in_maps = []
    for core in range(N_CORES):
        b, hf = core // halves, core % halves
        in_maps.append(core_inputs(inp, b, hf * NT_CORE, NT_CORE, NPRE=NT_CORE))
    res = run_bass_kernel_spmd(nc, in_maps, core_ids=list(range(N_CORES)))
    out = np.empty((B, S, D), np.float32)
    for core in range(N_CORES):
        b, hf = core // halves, core % halves
        out[b, hf * NT_CORE * 128:(hf + 1) * NT_CORE * 128] = res.results[core]["out"]
    return out
```
